# Optimizing a Trainium2 kernel written in Bass

```python
import math
import jax, jax.numpy as jnp
from jax import lax
import numpy as np

D_MODEL = 2048
BATCH = 8
SEQ = 4096
DEPTH = 2

GRID_W = 64
CTX_LEN = 256
D_POOL = D_MODEL // 2
D_ATTN = D_MODEL // 2
D_MIX = D_POOL + D_ATTN
POOL_WINDOWS = (2, 4, 8, 16)
N_POOL_GROUPS = len(POOL_WINDOWS)
POOL_GROUP = D_POOL // N_POOL_GROUPS
DIFF_HEAD_DIM = 64
N_DIFF_HEADS = D_ATTN // (2 * DIFF_HEAD_DIM)
V_HEAD_DIM = 2 * DIFF_HEAD_DIM
ROPE_PAIRS = DIFF_HEAD_DIM // 4
ROPE_BASE = 10000.0
Q_BLOCK = 128
EPS = 1e-6
O_POOL_V = 0
O_POOL_G = O_POOL_V + D_POOL
O_Q = O_POOL_G + D_POOL
O_K = O_Q + D_ATTN
O_V = O_K + D_ATTN
O_ATTN_G = O_V + D_ATTN
D_IN = O_ATTN_G + D_ATTN

kernel_name = 'hybrid_pool_diffattn_prefix_block'


def rms_norm(x, w):
    xf = x.astype(jnp.float32)
    y = xf * lax.rsqrt(jnp.mean(xf * xf, axis=-1, keepdims=True) + EPS)
    return (y * w.astype(jnp.float32)).astype(x.dtype)


def adaln_params(cond, w_ada, b_ada):
    mod = jax.nn.silu(cond) @ w_ada + b_ada
    shift, scale, gate = jnp.split(mod, 3, axis=-1)
    return shift[:, None, :], scale[:, None, :], gate[:, None, :]


def multiscale_pool(v, pool_w, pool_scale):
    B, L, _ = v.shape
    vf = v.astype(jnp.float32)
    csum = jnp.concatenate([jnp.zeros((B, 1, D_POOL), jnp.float32), jnp.cumsum(vf, axis=1)], axis=1)
    t = jnp.arange(L)
    outs = []
    for g, w in enumerate(POOL_WINDOWS):
        sl = slice(g * POOL_GROUP, (g + 1) * POOL_GROUP)
        lo = jnp.clip(t - w // 2, 0, L)
        hi = jnp.clip(t + w // 2, 0, L)
        cs = csum[:, :, sl]
        cnt = (hi - lo).astype(jnp.float32)[None, :, None]
        mean = (jnp.take(cs, hi, axis=1) - jnp.take(cs, lo, axis=1)) / cnt
        outs.append(mean - vf[:, :, sl])
    pooled = jnp.stack(outs, axis=2)
    mixed = jnp.einsum('blgc,gcd->blgd', pooled, pool_w.astype(jnp.float32))
    return (mixed.reshape(B, L, D_POOL) * pool_scale.astype(jnp.float32)).astype(v.dtype)


def axial_rope_tables(L, dtype):
    rows = L // GRID_W
    row = jnp.broadcast_to(jnp.arange(rows)[:, None], (rows, GRID_W)).reshape(-1).astype(jnp.float32)
    col = jnp.broadcast_to(jnp.arange(GRID_W)[None, :], (rows, GRID_W)).reshape(-1).astype(jnp.float32)
    inv_freq = ROPE_BASE ** (-jnp.arange(ROPE_PAIRS, dtype=jnp.float32) / ROPE_PAIRS)
    ang_r = row[:, None] * inv_freq
    ang_c = col[:, None] * inv_freq
    shape = (1, L, 1, 1, ROPE_PAIRS)
    cos_r = jnp.cos(ang_r).reshape(shape).astype(dtype)
    sin_r = jnp.sin(ang_r).reshape(shape).astype(dtype)
    cos_c = jnp.cos(ang_c).reshape(shape).astype(dtype)
    sin_c = jnp.sin(ang_c).reshape(shape).astype(dtype)
    return (cos_r, sin_r, cos_c, sin_c)


def rotate_half(x, cos, sin):
    x1, x2 = jnp.split(x, 2, axis=-1)
    return jnp.concatenate([x1 * cos - x2 * sin, x2 * cos + x1 * sin], axis=-1)


def apply_axial_rope(x, tables):
    cos_r, sin_r, cos_c, sin_c = tables
    half = DIFF_HEAD_DIM // 2
    return jnp.concatenate([rotate_half(x[..., :half], cos_r, sin_r),
                            rotate_half(x[..., half:], cos_c, sin_c)], axis=-1)


def diff_attend(q, k, v, lam):
    s = jnp.einsum('bqhnd,bkhnd->bhnqk', q, k).astype(jnp.float32) * (DIFF_HEAD_DIM ** -0.5)
    p = jax.nn.softmax(s, axis=-1)
    a = p[:, :, 0] - lam * p[:, :, 1]
    return jnp.einsum('bhqk,bkhe->bqhe', a.astype(v.dtype), v)


def merge_branches(p, o_attn, lam_init, pool_w, pool_scale, subln_w, w_out):
    B, L = p.shape[:2]
    pool_o = multiscale_pool(p[..., O_POOL_V:O_POOL_G], pool_w, pool_scale) * jax.nn.silu(p[..., O_POOL_G:O_Q])
    attn_o = (rms_norm(o_attn, subln_w).reshape(B, L, D_ATTN) * (1.0 - lam_init)
              * jax.nn.silu(p[..., O_ATTN_G:D_IN]))
    return jnp.concatenate([pool_o, attn_o], axis=-1) @ w_out


def hybrid_layer(x, ctx, c, c_ctx, rope_tables, layer_idx, update_ctx,
                 norm_w, w_ada, b_ada, w_in, pool_w, pool_scale, q_norm_w, k_norm_w,
                 lambda_q1, lambda_k1, lambda_q2, lambda_k2, subln_w, w_out):
    B, L, _ = x.shape
    C = ctx.shape[1]
    f32 = jnp.float32
    lam_init = 0.8 - 0.6 * math.exp(-0.3 * layer_idx)
    lam = (jnp.exp(jnp.sum(lambda_q1.astype(f32) * lambda_k1.astype(f32)))
           - jnp.exp(jnp.sum(lambda_q2.astype(f32) * lambda_k2.astype(f32))) + lam_init)

    shift, scale, gate = adaln_params(c, w_ada, b_ada)
    shift_c, scale_c, gate_c = adaln_params(c_ctx[None, :], w_ada, b_ada)
    h = rms_norm(x, norm_w) * (1 + scale) + shift
    hc = rms_norm(ctx, norm_w) * (1 + scale_c) + shift_c

    ctx_cols = slice(0, D_IN) if update_ctx else slice(O_K, O_ATTN_G)
    base = ctx_cols.start
    pc = hc @ w_in[:, ctx_cols]
    kc = rms_norm(pc[..., O_K - base:O_V - base].reshape(B, C, N_DIFF_HEADS, 2, DIFF_HEAD_DIM), k_norm_w)
    vc = pc[..., O_V - base:O_ATTN_G - base].reshape(B, C, N_DIFF_HEADS, V_HEAD_DIM)

    p = h @ w_in
    q = apply_axial_rope(rms_norm(p[..., O_Q:O_K].reshape(B, L, N_DIFF_HEADS, 2, DIFF_HEAD_DIM), q_norm_w), rope_tables)
    k = apply_axial_rope(rms_norm(p[..., O_K:O_V].reshape(B, L, N_DIFF_HEADS, 2, DIFF_HEAD_DIM), k_norm_w), rope_tables)
    v = p[..., O_V:O_ATTN_G].reshape(B, L, N_DIFF_HEADS, V_HEAD_DIM)
    k_all = jnp.concatenate([kc, k], axis=1)
    v_all = jnp.concatenate([vc, v], axis=1)
    nb = L // Q_BLOCK
    qb = q.reshape(B, nb, Q_BLOCK, N_DIFF_HEADS, 2, DIFF_HEAD_DIM).swapaxes(0, 1)
    o = lax.map(lambda qi: diff_attend(qi, k_all, v_all, lam), qb)
    o = o.swapaxes(0, 1).reshape(B, L, N_DIFF_HEADS, V_HEAD_DIM)
    x_new = x + gate * merge_branches(p, o, lam_init, pool_w, pool_scale, subln_w, w_out)

    if update_ctx:
        qc = rms_norm(pc[..., O_Q:O_K].reshape(B, C, N_DIFF_HEADS, 2, DIFF_HEAD_DIM), q_norm_w)
        oc = diff_attend(qc, kc, vc, lam)
        ctx_new = ctx + gate_c * merge_branches(pc, oc, lam_init, pool_w, pool_scale, subln_w, w_out)
    else:
        ctx_new = ctx
    return x_new, ctx_new


def setup_inputs(seed: int = 0) -> dict:
    key = jax.random.key(seed)
    ks = jax.random.split(key, 18)
    f32 = jnp.float32

    def nrm(k, shape, s):
        return jax.random.normal(k, shape, f32) * s

    def gain(k, shape):
        return 1.0 + 0.02 * jax.random.normal(k, shape, f32)

    return {
        'x': nrm(ks[0], (BATCH, SEQ, D_MODEL), 1.0),
        'c': nrm(ks[1], (BATCH, D_MODEL), 1.0),
        'ctx': nrm(ks[2], (BATCH, CTX_LEN, D_MODEL), 1.0),
        'c_ctx': nrm(ks[3], (D_MODEL,), 1.0),
        'norm_w': gain(ks[4], (DEPTH, D_MODEL)),
        'w_ada': nrm(ks[5], (DEPTH, D_MODEL, 3 * D_MODEL), D_MODEL ** -0.5),
        'b_ada': nrm(ks[6], (DEPTH, 3 * D_MODEL), 0.02),
        'w_in': nrm(ks[7], (DEPTH, D_MODEL, D_IN), D_MODEL ** -0.5),
        'pool_w': nrm(ks[8], (DEPTH, N_POOL_GROUPS, POOL_GROUP, POOL_GROUP), POOL_GROUP ** -0.5),
        'pool_scale': gain(ks[9], (DEPTH, D_POOL)),
        'q_norm_w': gain(ks[10], (DEPTH, DIFF_HEAD_DIM)),
        'k_norm_w': gain(ks[11], (DEPTH, DIFF_HEAD_DIM)),
        'lambda_q1': nrm(ks[12], (DEPTH, DIFF_HEAD_DIM), 0.1),
        'lambda_k1': nrm(ks[13], (DEPTH, DIFF_HEAD_DIM), 0.1),
        'lambda_q2': nrm(ks[14], (DEPTH, DIFF_HEAD_DIM), 0.1),
        'lambda_k2': nrm(ks[15], (DEPTH, DIFF_HEAD_DIM), 0.1),
        'subln_w': gain(ks[16], (DEPTH, V_HEAD_DIM)),
        'w_out': nrm(ks[17], (DEPTH, D_MIX, D_MODEL), D_MIX ** -0.5),
    }


def reference(x, c, ctx, c_ctx, norm_w, w_ada, b_ada, w_in, pool_w, pool_scale,
              q_norm_w, k_norm_w, lambda_q1, lambda_k1, lambda_q2, lambda_k2, subln_w, w_out):
    rope_tables = axial_rope_tables(x.shape[1], x.dtype)
    for l in range(DEPTH):
        x, ctx = hybrid_layer(x, ctx, c, c_ctx, rope_tables, l, l < DEPTH - 1,
                              norm_w[l], w_ada[l], b_ada[l], w_in[l], pool_w[l], pool_scale[l],
                              q_norm_w[l], k_norm_w[l], lambda_q1[l], lambda_k1[l],
                              lambda_q2[l], lambda_k2[l], subln_w[l], w_out[l])
    return x
```

```python
import math
from contextlib import ExitStack

import numpy as np

import concourse.bass as bass
import concourse.mybir as mybir
from concourse.bass_utils import run_bass_kernel_spmd

F32 = mybir.dt.float32
BF16 = mybir.dt.bfloat16
ALU = mybir.AluOpType
AF = mybir.ActivationFunctionType
AX = mybir.AxisListType

D = 2048
KC = 16
L = 4096
C = 256
LT = L + C
NT = LT // 128
H = 8
DIN = 6144
EPS = 1e-6
WINS = (2, 4, 8, 16)
NLAYERS = 2

DEBUG_SCRATCH = False
STOP_AFTER = None
N_DSEMS = 60


def lam_init_of(l):
    return 0.8 - 0.6 * math.exp(-0.3 * l)


_UNIQ = [0]


def SBT(nc, name, shape, dt):
    _UNIQ[0] += 1
    return nc.sbuf_tensor("%s_s%d" % (name, _UNIQ[0]), shape, dt)


def PST(nc, name, shape, dt):
    _UNIQ[0] += 1
    return nc.psum_tensor("%s_p%d" % (name, _UNIQ[0]), shape, dt)


class Sem:
    def __init__(self, h):
        self.h = h
        self.val = 0


class Buf:
    def __init__(self, name, t=None, dsem=None):
        self.name = name
        self.t = t
        self.dsem = dsem
        self.w = {}
        self.r = {}


class Queue:
    def __init__(self, name):
        self.name = name
        self.sem = None
        self.ops = []
        self.seen = {}


def _merge(dst, src):
    for s, v in src.items():
        if dst.get(s, 0) < v:
            dst[s] = v


class Prog:
    def __init__(self, nc, es):
        self.nc = nc
        self.qs = {n: Queue(n) for n in ("pe", "act", "dve", "pool", "sp")}
        for n in ("pe", "act", "dve"):
            self.qs[n].sem = Sem(es.enter_context(nc.semaphore("c_" + n)))
        self.dsems = [Sem(es.enter_context(nc.semaphore("d%d" % i))) for i in range(N_DSEMS)]
        self.dnext = 0
        self.nops = 0

    def new_dsem(self):
        s = self.dsems[self.dnext]
        self.dnext += 1
        return s

    def phase_reset(self):
        self.dnext = 0

    def op(self, qn, fn, R=(), W=(), dsem=None):
        q = self.qs[qn]
        need = {}
        for b in R:
            _merge(need, b.w)
        for b in W:
            _merge(need, b.w)
            _merge(need, b.r)
        waits = []
        for s, v in need.items():
            if q.seen.get(s, 0) < v:
                waits.append((s, v))
                q.seen[s] = v
        if dsem is not None:
            sem = dsem
            sem.val += 16
            inc = 16
        else:
            sem = q.sem
            sem.val += 1
            inc = 1
        val = sem.val
        for b in R:
            if b.r.get(sem, 0) < val:
                b.r[sem] = val
        for b in W:
            b.w = {sem: val}
            b.r = {}
        q.ops.append((waits, fn, sem, inc))
        self.nops += 1

    def barrier(self):
        sems = [self.qs[n].sem for n in ("pe", "act", "dve")] + self.dsems
        for q in self.qs.values():
            waits = []
            for s in sems:
                if s.val > q.seen.get(s, 0):
                    waits.append((s, s.val))
                    q.seen[s] = s.val
            q.ops.append((waits, None, None, 0))

    def emit(self):
        nc = self.nc
        with nc.Block() as block:
            for name, deco in (("sp", block.sync), ("act", block.scalar), ("dve", block.vector),
                               ("pool", block.gpsimd), ("pe", block.tensor)):
                q = self.qs[name]
                ops = q.ops
                q.ops = []

                def body(eng, ops=ops):
                    for waits, fn, sem, inc in ops:
                        for s, v in waits:
                            eng.wait_ge(s.h, v)
                        if fn is not None:
                            ins = fn(eng)
                            ins.then_inc(sem.h, inc)

                deco(body)

    def tt(self, out, in0, in1, op, R, W, q="dve"):
        self.op(q, lambda e: e.tensor_tensor(out=out, in0=in0, in1=in1, op=op), R, W)

    def ts(self, out, in0, s1, s2, op0, op1, R, W, q="dve"):
        self.op(q, lambda e: e.tensor_scalar(out=out, in0=in0, scalar1=s1, scalar2=s2, op0=op0, op1=op1), R, W)

    def stt(self, out, in0, scalar, in1, op0, op1, R, W, q="dve"):
        self.op(q, lambda e: e.scalar_tensor_tensor(out=out, in0=in0, scalar=scalar, in1=in1, op0=op0, op1=op1), R, W)

    def copy(self, out, in_, R, W, q="dve"):
        self.op(q, lambda e: e.tensor_copy(out=out, in_=in_), R, W)

    def red(self, out, in_, R, W, q="dve"):
        self.op(q, lambda e: e.tensor_reduce(out=out, in_=in_, axis=AX.X, op=ALU.add), R, W)

    def recip(self, out, in_, R, W):
        self.op("dve", lambda e: e.reciprocal(out=out, in_=in_), R, W)

    def memset(self, ap, val, R, W, q="dve"):
        self.op(q, lambda e: e.memset(ap, val), R, W)

    def act(self, out, in_, func, R, W, scale=1.0, bias=0.0, accum=None):
        if accum is None:
            self.op("act", lambda e: e.activation(out=out, in_=in_, func=func, scale=scale, bias=bias), R, W)
        else:
            self.op("act", lambda e: e.activation(out=out, in_=in_, func=func, scale=scale, bias=bias,
                                                  accum_out=accum), R, W)

    def dma(self, q, out, in_, buf, R, W):
        self.op(q, lambda e: e.dma_start(out=out, in_=in_), R, W, dsem=buf.dsem)

    def dma_multi(self, q, pairs, buf, R, W):
        sem = buf.dsem
        n = len(pairs)

        def fn(e):
            ins = None
            for i, (o, a) in enumerate(pairs):
                ins = e.dma_start(out=o, in_=a)
                if i < n - 1:
                    ins.then_inc(sem.h, 16)
            return ins
        sem.val += 16 * (n - 1)
        self.op(q, fn, R, W, dsem=sem)

    def mm(self, out, pairs, R, W, start=True, stop=True):
        def fn(e):
            n = len(pairs)
            ins = None
            for i, (a, b) in enumerate(pairs):
                ins = e.matmul(out, lhsT=a, rhs=b, start=(start and i == 0), stop=(stop and i == n - 1))
            return ins
        self.op("pe", fn, R, W)

    def mm_multi(self, items, R, W):
        def fn(e):
            ins = None
            for (o, a, b, st, sp, tp) in items:
                if tp is None:
                    ins = e.matmul(o, lhsT=a, rhs=b, start=st, stop=sp)
                else:
                    ins = e.matmul(o, lhsT=a, rhs=b, start=st, stop=sp, tile_position=tp)
            return ins
        self.op("pe", fn, R, W)

    def transposes(self, items, ident, R, W):
        def fn(e):
            ins = None
            for (o, i) in items:
                ins = e.transpose(out=o, in_=i, identity=ident)
            return ins
        self.op("pe", fn, R, W)


class Ring:
    def __init__(self, prog, es, name, n, shape, dtype, dma=False, psum=False):
        nc = prog.nc
        self.slots = []
        for i in range(n):
            if psum:
                t = es.enter_context(PST(nc, "%s%d" % (name, i), shape, dtype))
            else:
                t = es.enter_context(SBT(nc, "%s%d" % (name, i), shape, dtype))
            self.slots.append(Buf("%s%d" % (name, i), t, prog.new_dsem() if dma else None))
        self.i = 0
        self.n = n

    def next(self):
        b = self.slots[self.i % self.n]
        self.i += 1
        return b


def bc(ap, shape):
    return ap.to_broadcast(shape)


def build():
    nc = bass.Bass("TRN2", target_bir_lowering=False)
    skind = "ExternalOutput" if DEBUG_SCRATCH else "Internal"

    def din(name, shape, dt=F32):
        return nc.dram_tensor(name, shape, dt, kind="ExternalInput").ap()

    def dscr(name, shape, dt):
        return nc.dram_tensor(name, shape, dt, kind=skind).ap()

    x_in = din("x", [L, D])
    ctx_in = din("ctx", [C, D])
    cond_in = din("cond", [128, 32])
    norm_w = din("norm_w", [NLAYERS, D])
    w_ada = din("w_ada", [NLAYERS, D, DIN])
    b_ada = din("b_ada", [NLAYERS, DIN])
    w_in = din("w_in", [NLAYERS, D, DIN])
    pool_w = din("pool_w", [NLAYERS, 4, 256, 256])
    psc_in = din("pscT", [128, 16])
    qw_in = din("qw", [NLAYERS * 64])
    kw_in = din("kw", [NLAYERS * 64])
    lams_in = din("lams", [NLAYERS * 4 * 64])
    sw_in = din("swT", [128, NLAYERS])
    w_out = din("w_out", [NLAYERS, D, D])
    ident_in = din("ident", [128, 128])
    cos_in = din("rope_cos", [128, NT * 32])
    sin_in = din("rope_sin", [128, NT * 32])
    edge_in = din("edge_rc", [64])

    y = nc.dram_tensor("y", [L, D], F32, kind="ExternalOutput").ap()

    modd = dscr("modd", [NLAYERS, 2, DIN], F32)
    qT = dscr("qT", [H, 128, L], BF16)
    qcT = dscr("qcT", [H, 128, C], BF16)
    kT = dscr("kT", [H, 128, LT], BF16)
    Vs = dscr("Vs", [H, 128, NT, 128], BF16)
    pvT = dscr("pvT", [8, 128, LT], BF16)
    pgT = dscr("pgT", [8, 128, LT], BF16)
    agT = dscr("agT", [8, 128, LT], BF16)
    mT = dscr("mT", [16, 128, LT], BF16)
    ctx1 = dscr("ctx1", [C, D], F32)
    w_in_bf = nc.dram_tensor("w_in_bf", [NLAYERS, D, DIN], BF16, kind="Internal").ap()
    w_out_bf = nc.dram_tensor("w_out_bf", [NLAYERS, D, D], BF16, kind="Internal").ap()

    with ExitStack() as top:
        P = Prog(nc, top)
        E = top.enter_context

        def pers(name, shape, dt, dma=False):
            t = E(SBT(nc, name, shape, dt))
            return Buf(name, t, P.new_dsem() if dma else None)

        identf = pers("identf", [128, 128], F32, dma=True)
        identb = pers("identb", [128, 128], BF16)
        onesb = pers("onesb", [128, 128], BF16)
        cosT = pers("cosT", [128, NT * 32], F32, dma=True)
        sinT = pers("sinT", [128, NT * 32], F32, dma=True)
        edge = pers("edge", [128, 64], F32, dma=True)
        psc = pers("psc", [128, 16], F32, dma=True)
        sws = pers("sws", [128, NLAYERS], F32, dma=True)
        qw = pers("qw", [128, NLAYERS * 64], F32, dma=True)
        kw = pers("kw", [128, NLAYERS * 64], F32, dma=True)
        lamt = pers("lamt", [128, NLAYERS * 4 * 64], F32, dma=True)
        lpr = pers("lpr", [128, 4 * 64], F32)
        ls4 = pers("ls4", [128, 4], F32)
        le4 = pers("le4", [128, 4], F32)
        nl0 = pers("nl0", [128, NLAYERS], F32)
        neglam = pers("neglam", [128, NLAYERS], F32)
        wbf_in = [Buf("wbf_in%d" % l_, None, P.new_dsem()) for l_ in range(NLAYERS)]
        wbf_out = [Buf("wbf_out%d" % l_, None, P.new_dsem()) for l_ in range(NLAYERS)]
        n_pers_dsems = P.dnext

        def conv_in(l_):
            P.dma_multi("pool", [(w_in_bf[l_][r * 128:(r + 1) * 128, :], w_in[l_][r * 128:(r + 1) * 128, :])
                                 for r in range(KC)], wbf_in[l_], [], [wbf_in[l_]])

        def conv_out(l_):
            P.dma_multi("pool", [(w_out_bf[l_][r * 128:(r + 1) * 128, :], w_out[l_][r * 128:(r + 1) * 128, :])
                                 for r in range(KC)], wbf_out[l_], [], [wbf_out[l_]])


        P.dma("sp", identf.t[:], ident_in, identf, [], [identf])
        P.dma("sp", cosT.t[:], cos_in, cosT, [], [cosT])
        P.dma("sp", sinT.t[:], sin_in, sinT, [], [sinT])
        P.dma("sp", edge.t[:], edge_in.partition_broadcast(128), edge, [], [edge])
        P.dma("sp", psc.t[:], psc_in, psc, [], [psc])
        P.dma("sp", sws.t[:], sw_in, sws, [], [sws])
        P.dma("sp", qw.t[:], qw_in.partition_broadcast(128), qw, [], [qw])
        P.dma("sp", kw.t[:], kw_in.partition_broadcast(128), kw, [], [kw])
        P.dma("sp", lamt.t[:], lams_in.partition_broadcast(128), lamt, [], [lamt])
        selb = pers("selb", [64, 256], BF16)
        P.memset(selb.t[:], 0.0, [], [selb])
        P.memset(selb.t[0:1, 0:128], 1.0, [selb], [selb])
        P.memset(selb.t[32:33, 128:256], 1.0, [selb], [selb])
        P.copy(identb.t[:], identf.t[:], [identf], [identb])
        P.memset(onesb.t[:], 1.0, [], [onesb])
        for l in range(NLAYERS):
            P.ts(sws.t[:, l:l + 1], sws.t[:, l:l + 1], float(1.0 - lam_init_of(l)), 0.0, ALU.mult, ALU.add,
                 [sws], [sws])
        lv = lamt.t[:].rearrange("p (a j d) -> p a j d", j=2, d=64)
        P.tt(lpr.t[:].rearrange("p (a d) -> p a d", d=64), lv[:, :, 0, :], lv[:, :, 1, :], ALU.mult, [lamt], [lpr])
        P.red(ls4.t[:], lpr.t[:].rearrange("p (a d) -> p a d", d=64), [lpr], [ls4])
        P.act(le4.t[:], ls4.t[:], AF.Exp, [ls4], [le4])
        for l in range(NLAYERS):
            P.tt(nl0.t[:, l:l + 1], le4.t[:, 2 * l + 1:2 * l + 2], le4.t[:, 2 * l:2 * l + 1], ALU.subtract,
                 [le4], [nl0])
            P.ts(neglam.t[:, l:l + 1], nl0.t[:, l:l + 1], 1.0, float(-lam_init_of(l)), ALU.mult, ALU.add,
                 [nl0], [neglam])

        def end_phase():
            P.barrier()
            P.emit()
            P.dnext = n_pers_dsems

        with ExitStack() as ph:
            cs = Buf("cs", ph.enter_context(SBT(nc, "cs", [128, 32], F32)), P.new_dsem())
            e1 = Buf("e1", ph.enter_context(SBT(nc, "e1", [128, 32], F32)))
            scb = Buf("scb", ph.enter_context(SBT(nc, "scb", [128, 32], BF16)))
            bsb = Buf("bsb", ph.enter_context(SBT(nc, "bsb", [2, DIN], F32)), P.new_dsem())
            modsb = Buf("modsb", ph.enter_context(SBT(nc, "modsb", [2, DIN], F32)), P.new_dsem())
            wa = Ring(P, ph, "wa", 3, [128, KC, 512], BF16, dma=True)
            pmod = Ring(P, ph, "pmod", 2, [128, 512], F32, psum=True)
            P.dma("sp", cs.t[:], cond_in, cs, [], [cs])
            P.act(e1.t[:], cs.t[:], AF.Exp, [cs], [e1], scale=-1.0)
            P.act(e1.t[:], e1.t[:], AF.Ln, [e1], [e1], scale=1.0, bias=1.0)
            P.act(e1.t[:], e1.t[:], AF.Exp, [e1], [e1], scale=-1.0)
            P.tt(scb.t[:], cs.t[:], e1.t[:], ALU.mult, [cs, e1], [scb])
            for l in range(NLAYERS):
                P.dma("sp", bsb.t[:], b_ada[l].partition_broadcast(2), bsb, [], [bsb])
                for n in range(DIN // 512):
                    w = wa.next()
                    wsrc = w_ada[l][:, n * 512:(n + 1) * 512].rearrange("(k p) n -> p k n", p=128)
                    P.dma_multi("pool", [(w.t[:, 4 * j:4 * j + 4, :], wsrc[:, 4 * j:4 * j + 4, :]) for j in range(4)],
                                w, [], [w])
                    pm = pmod.next()
                    P.mm(pm.t[0:2, :], [(scb.t[:, 2 * k:2 * k + 2], w.t[:, k, :]) for k in range(KC)],
                         [scb, w], [pm])
                    P.tt(modsb.t[0:2, n * 512:(n + 1) * 512], pm.t[0:2, :], bsb.t[0:2, n * 512:(n + 1) * 512],
                         ALU.add, [pm, bsb], [modsb])
                P.dma("sp", modd[l], modsb.t[:], modsb, [modsb], [])
                if l == 0:
                    conv_in(0)
            end_phase()
        if STOP_AFTER == (0, "P0"):
            return nc

        for l in range(NLAYERS):
            x_src = x_in if l == 0 else y
            c_src = ctx_in if l == 0 else ctx1
            with_ctx = (l == 0)
            phase_A(nc, P, l, x_src, c_src, with_ctx, locals())
            end_phase()
            if STOP_AFTER == (l, "A"):
                return nc
            if l == 0:
                conv_out(0)
                for l2 in range(1, NLAYERS):
                    conv_in(l2)
                    conv_out(l2)
            phase_attn(nc, P, l, with_ctx, locals())
            end_phase()
            if STOP_AFTER == (l, "T"):
                return nc
            phase_out(nc, P, l, x_src, c_src, with_ctx, locals())
            end_phase()
            if STOP_AFTER == (l, "O"):
                return nc
    return nc


def blocks_of_tiles(with_ctx_full):
    blks = [[0, 1]]
    for j in range(8):
        blks.append([2 + 4 * j + i for i in range(4)])
    return blks


def phase_A(nc, P, l, x_src, c_src, with_ctx, G):
    modd, norm_w, w_in = G["modd"], G["norm_w"], G["w_in"]
    identb, cosT, sinT, qw, kw = G["identb"], G["cosT"], G["sinT"], G["qw"], G["kw"]
    pvT, pgT, agT, qT, qcT, kT, Vs = G["pvT"], G["pgT"], G["agT"], G["qT"], G["qcT"], G["kT"], G["Vs"]
    with ExitStack() as ph:
        S = ph.enter_context

        def sb(name, shape, dt, dma=False):
            return Buf(name, S(SBT(nc, name, shape, dt)), P.new_dsem() if dma else None)

        Gx = sb("Gx", [128, D], F32, True)
        Sx = sb("Sx", [128, D], F32, True)
        Gc = sb("Gc", [128, D], F32, True)
        Sc = sb("Sc", [128, D], F32, True)
        junk = S(SBT(nc, "junk", [128, D], BF16))
        xring = Ring(P, ph, "xr", 2, [128, D], F32, dma=True)
        ssq = Ring(P, ph, "ssq", 4, [128, 1], F32)
        rstd = Ring(P, ph, "rstd", 4, [128, 1], F32)
        tmp = Ring(P, ph, "tmp", 1, [128, D], F32, dma=True)
        hb = Ring(P, ph, "hb", 2, [128, D], BF16)
        pT = Ring(P, ph, "pT", 1, [128, D], BF16, psum=True)
        hT = [S(SBT(nc, "hT%d" % i, [128, KC, 512], BF16)) for i in range(2)]
        hTb = [[Buf("hT%d_%d" % (i, j)) for j in range(4)] for i in range(2)]
        wsl = Ring(P, ph, "wsl", 2, [128, KC, 512], BF16, dma=True)
        pmm = Ring(P, ph, "pmm", 4, [128, 512], F32, psum=True)
        pvst = Ring(P, ph, "pvst", 2, [128, 512], BF16, dma=True)
        gt = Ring(P, ph, "gt", 2, [128, 512], F32)
        gst = Ring(P, ph, "gst", 2, [128, 512], BF16, dma=True)
        vst = Ring(P, ph, "vst", 2, [128, 512], BF16, dma=True)
        sqr = Ring(P, ph, "sqr", 2, [128, 512], F32)
        ss8 = Ring(P, ph, "ss8", 2, [128, 8], F32)
        rs8 = Ring(P, ph, "rs8", 2, [128, 8], F32)
        xn = Ring(P, ph, "xn", 2, [128, 512], F32)
        rtmps = [Ring(P, ph, "rtmp%d" % a, 2, [128, 256], F32) for a in range(4)]
        qn = Ring(P, ph, "qn", 4, [128, 512], BF16)
        pq = Ring(P, ph, "pq", 1, [128, 512], BF16, psum=True)
        qst = Ring(P, ph, "qst", 2, [128, 512], BF16, dma=True)

        nwb = tmp.next()
        P.dma("sp", nwb.t[:], norm_w[l].partition_broadcast(128), nwb, [], [nwb])
        for (Gb, Sb_, r) in ((Gx, Sx, 0), (Gc, Sc, 1)):
            P.dma("sp", Sb_.t[:], modd[l, r, 0:D].partition_broadcast(128), Sb_, [], [Sb_])
            P.dma("sp", Gb.t[:], modd[l, r, D:2 * D].partition_broadcast(128), Gb, [], [Gb])
            P.stt(Gb.t[:], Gb.t[:], 1.0, nwb.t[:], ALU.add, ALU.mult, [Gb, nwb], [Gb])

        blks = blocks_of_tiles(True)

        def prep1(T):
            xt = xring.next()
            src = c_src[T * 128:(T + 1) * 128, :] if T < 2 else x_src[(T - 2) * 128:(T - 1) * 128, :]
            P.dma("sp", xt.t[:], src, xt, [], [xt])
            sq_ = ssq.next()
            P.memset(sq_.t[:], 0.0, [], [sq_])
            P.act(junk[:], xt.t[:], AF.Square, [xt, sq_], [sq_], accum=sq_.t[:])
            rs = rstd.next()
            P.act(rs.t[:], sq_.t[:], AF.Ln, [sq_], [rs], scale=1.0 / D, bias=EPS)
            P.act(rs.t[:], rs.t[:], AF.Exp, [rs], [rs], scale=-0.5)
            G_, S_ = (Gc, Sc) if T < 2 else (Gx, Sx)
            tm = tmp.next()
            P.stt(tm.t[:], xt.t[:], rs.t[:, 0:1], G_.t[:], ALU.mult, ALU.mult, [xt, rs, G_], [tm])
            hb_ = hb.next()
            P.tt(hb_.t[:], tm.t[:], S_.t[:], ALU.add, [tm, S_], [hb_])
            return hb_

        def prep2(hb_, slot, pos):
            pt = pT.next()
            P.transposes([(pt.t[:, k * 128:(k + 1) * 128], hb_.t[:, k * 128:(k + 1) * 128]) for k in range(KC)],
                         identb.t[:], [hb_, identb], [pt])
            P.act(hT[slot][:, :, pos * 128:(pos + 1) * 128], pt.t[:].rearrange("p (k t) -> p k t", t=128),
                  AF.Copy, [pt], [hTb[slot][pos]])

        deferred = []
        DEPTH = 2

        def drain(keep):
            while len(deferred) > keep:
                deferred.pop(0)()

        kinds = ["pv", "pv", "pg", "pg", "q", "q", "k", "k", "v", "v", "ag", "ag"]
        for pos, T in enumerate(blks[0]):
            prep2(prep1(T), 0, pos)
        for bi, tiles in enumerate(blks):
            slot = bi % 2
            N = 128 * len(tiles)
            tok0 = tiles[0] * 128
            is_ctx = (bi == 0)
            cbs = list(range(12)) if (with_ctx or not is_ctx) else [6, 7, 8, 9]
            nxt = blks[bi + 1] if bi + 1 < len(blks) else []
            events = {}
            hb_of = {}
            for i, Tn in enumerate(nxt):
                p1 = min(len(cbs) - 1, 2 * i)
                p2 = min(len(cbs) - 1, 2 * i + 2)
                events.setdefault(p1, []).append((float(i), lambda Tn=Tn, i=i: hb_of.__setitem__(i, prep1(Tn))))
                events.setdefault(p2, []).append((i + 1.5, lambda i=i, s_=1 - slot: prep2(hb_of[i], s_, i)))
            hbufs = hTb[slot][:len(tiles)]
            for ci, cb in enumerate(cbs):
                w = wsl.next()
                wsrc = G["w_in_bf"][l][:, cb * 512:(cb + 1) * 512].rearrange("(k p) n -> p k n", p=128)
                P.dma_multi("pool", [(w.t[:, 4 * j:4 * j + 4, :], wsrc[:, 4 * j:4 * j + 4, :]) for j in range(4)],
                            w, [G["wbf_in"][l]], [w])
                kind = kinds[cb]
                if kind in ("pv", "pg", "ag"):
                    for sub in range(4):
                        c = (cb % 2) * 4 + sub
                        drain(DEPTH)
                        ps = pmm.next()
                        P.mm(ps.t[:, 0:N], [(w.t[:, k, sub * 128:(sub + 1) * 128], hT[slot][:, k, 0:N])
                                            for k in range(KC)], [w] + hbufs, [ps])
                        if kind == "pv":
                            st = pvst.next()
                            P.act(st.t[:, 0:N], ps.t[:, 0:N], AF.Copy, [ps], [st])
                            P.dma("sp", pvT[c, :, tok0:tok0 + N], st.t[:, 0:N], st, [st], [])
                        else:
                            g1 = gt.next()
                            P.act(g1.t[:, 0:N], ps.t[:, 0:N], AF.Exp, [ps], [g1], scale=-1.0)
                            P.act(g1.t[:, 0:N], g1.t[:, 0:N], AF.Ln, [g1], [g1], scale=1.0, bias=1.0)
                            P.act(g1.t[:, 0:N], g1.t[:, 0:N], AF.Exp, [g1], [g1], scale=-1.0)
                            st = gst.next()
                            P.tt(st.t[:, 0:N], ps.t[:, 0:N], g1.t[:, 0:N], ALU.mult, [ps, g1], [st])
                            dst = pgT if kind == "pg" else agT
                            P.dma("sp", dst[c, :, tok0:tok0 + N], st.t[:, 0:N], st, [st], [])
                else:
                    h0 = (cb % 2) * 4
                    for pos, T in enumerate(tiles):
                        drain(DEPTH)
                        ps = pmm.next()
                        P.mm(ps.t[:, :], [(hT[slot][:, k, pos * 128:(pos + 1) * 128], w.t[:, k, :])
                                          for k in range(KC)], [w, hTb[slot][pos]], [ps])
                        if kind == "v":
                            st = vst.next()
                            P.act(st.t[:], ps.t[:], AF.Copy, [ps], [st])
                            P.dma("sp", Vs[h0:h0 + 4, :, T, :].rearrange("h p e -> p h e"),
                                  st.t[:].rearrange("p (h e) -> p h e", e=128), st, [st], [])
                            continue
                        wv = (qw if kind == "q" else kw)
                        sq_ = sqr.next()
                        P.act(sq_.t[:], ps.t[:], AF.Square, [ps], [sq_])
                        s8 = ss8.next()
                        P.red(s8.t[:], sq_.t[:].rearrange("p (g d) -> p g d", d=64), [sq_], [s8])
                        r8 = rs8.next()
                        P.act(r8.t[:], s8.t[:], AF.Ln, [s8], [r8], scale=1.0 / 64, bias=EPS)
                        P.act(r8.t[:], r8.t[:], AF.Exp, [r8], [r8], scale=-0.5)
                        xn_ = xn.next()
                        xg = xn_.t[:].rearrange("p (g d) -> p g d", d=64)
                        P.tt(xg, ps.t[:].rearrange("p (g d) -> p g d", d=64),
                             bc(r8.t[:].unsqueeze(2), [128, 8, 64]), ALU.mult, [ps, r8], [xn_])
                        P.tt(xg, xg, bc(wv.t[:, l * 64:(l + 1) * 64].unsqueeze(1), [128, 8, 64]), ALU.mult,
                             [xn_, wv], [xn_])
                        xv = xn_.t[:].rearrange("p (s h t i) -> p s h t i", s=8, h=2, t=2, i=16)
                        x1 = xv[:, :, :, 0, :]
                        x2 = xv[:, :, :, 1, :]
                        cs_ = bc(cosT.t[:, T * 32:(T + 1) * 32].rearrange("p (h i) -> p h i", i=16).unsqueeze(1),
                                 [128, 8, 2, 16])
                        sn_ = bc(sinT.t[:, T * 32:(T + 1) * 32].rearrange("p (h i) -> p h i", i=16).unsqueeze(1),
                                 [128, 8, 2, 16])
                        rts = [r_.next() for r_ in rtmps]
                        rv = [r_.t[:].rearrange("p (s h i) -> p s h i", s=8, h=2, i=16) for r_ in rts]
                        P.tt(rv[0], x1, cs_, ALU.mult, [xn_, cosT], [rts[0]])
                        P.tt(rv[1], x2, cs_, ALU.mult, [xn_, cosT], [rts[1]])
                        P.tt(rv[2], x2, sn_, ALU.mult, [xn_, sinT], [rts[2]])
                        P.tt(rv[3], x1, sn_, ALU.mult, [xn_, sinT], [rts[3]])
                        qn_ = qn.next()
                        qv = qn_.t[:].rearrange("p (s h t i) -> p s h t i", s=8, h=2, t=2, i=16)
                        P.tt(qv[:, :, :, 0, :], rv[0], rv[2], ALU.subtract, [rts[0], rts[2]], [qn_])
                        P.tt(qv[:, :, :, 1, :], rv[1], rv[3], ALU.add, [rts[1], rts[3]], [qn_])
                        if kind == "q":
                            if T < 2:
                                dst = qcT[h0:h0 + 4, :, T * 128:(T + 1) * 128]
                            else:
                                dst = qT[h0:h0 + 4, :, (T - 2) * 128:(T - 1) * 128]
                        else:
                            dst = kT[h0:h0 + 4, :, T * 128:(T + 1) * 128]

                        def stage(qn_=qn_, dst=dst):
                            pq_ = pq.next()
                            P.transposes([(pq_.t[:, hh * 128:(hh + 1) * 128], qn_.t[:, hh * 128:(hh + 1) * 128])
                                          for hh in range(4)], identb.t[:], [qn_, identb], [pq_])
                            st = qst.next()
                            P.act(st.t[:], pq_.t[:], AF.Copy, [pq_], [st])
                            P.dma("sp", dst.rearrange("h p t -> p h t"),
                                  st.t[:].rearrange("p (h t) -> p h t", t=128), st, [st], [])

                        deferred.append(stage)
                for _key, fn_ in sorted(events.get(ci, []), key=lambda e_: e_[0]):
                    fn_()
        drain(0)


def phase_pool(nc, P, l, with_ctx, G):
    pool_w, pvT, pgT, mT, edge, psc = G["pool_w"], G["pvT"], G["pgT"], G["mT"], G["edge"], G["psc"]
    with ExitStack() as ph:
        S = ph.enter_context
        WP = L + 16
        pwb = Buf("pwb", S(SBT(nc, "pwb", [128, 4, 2, 256], BF16)), P.new_dsem())
        vb = Ring(P, ph, "vb", 2, [128, L], BF16, dma=True)
        vp = Ring(P, ph, "vp", 2, [128, WP], F32)
        sr = Ring(P, ph, "sr", 3, [128, WP], F32)
        plr = Ring(P, ph, "plr", 4, [128, L], BF16)
        pgr = Ring(P, ph, "pgr", 2, [128, L], BF16, dma=True)
        pst = Ring(P, ph, "pst", 2, [128, 512], BF16, dma=True)
        etr = Ring(P, ph, "etr", 2, [128, 16], F32)
        pp = Ring(P, ph, "pp", 4, [128, 512], F32, psum=True)
        pwsrc = pool_w[l].rearrange("g (c p) d -> p g c d", p=128)
        P.dma_multi("pool", [(pwb.t[:, g_], pwsrc[:, g_]) for g_ in range(4)], pwb, [], [pwb])
        segs = ([(0, C)] if with_ctx else []) + [(C, L)]
        for g in range(4):
            w = WINS[g]
            for (t0, Ls) in segs:
                Wd = Ls + 16
                pls = []
                for cc in range(2):
                    c = g * 2 + cc
                    v = vb.next()
                    P.dma("sp", v.t[:, 0:Ls], pvT[c, :, t0:t0 + Ls], v, [], [v])
                    vp_ = vp.next()
                    P.memset(vp_.t[:, 0:8], 0.0, [], [vp_])
                    P.memset(vp_.t[:, 8 + Ls:16 + Ls], 0.0, [], [vp_])
                    P.copy(vp_.t[:, 8:8 + Ls], v.t[:, 0:Ls], [v, vp_], [vp_])
                    cur = vp_
                    ln = Wd
                    for s in range(g + 1):
                        sh = 1 << s
                        nx = sr.next()
                        ln = ln - sh
                        P.tt(nx.t[:, 0:ln], cur.t[:, 0:ln], cur.t[:, sh:sh + ln], ALU.add, [cur], [nx])
                        cur = nx
                    off = 8 - w // 2
                    pl_ = plr.next()
                    P.stt(pl_.t[:, 0:Ls], cur.t[:, off:off + Ls], 1.0 / w, vp_.t[:, 8:8 + Ls], ALU.mult, ALU.subtract,
                          [cur, vp_], [pl_])
                    et = etr.next()
                    P.tt(et.t[:, 0:8], cur.t[:, off:off + 8], edge.t[:, g * 16:g * 16 + 8], ALU.mult,
                         [cur, edge], [et])
                    P.tt(et.t[:, 8:16], cur.t[:, off + Ls - 8:off + Ls], edge.t[:, g * 16 + 8:g * 16 + 16], ALU.mult,
                         [cur, edge, et], [et])
                    P.tt(pl_.t[:, 0:8], et.t[:, 0:8], vp_.t[:, 8:16], ALU.subtract, [et, vp_, pl_], [pl_])
                    P.tt(pl_.t[:, Ls - 8:Ls], et.t[:, 8:16], vp_.t[:, Ls:Ls + 8], ALU.subtract, [et, vp_, pl_], [pl_])
                    pls.append(pl_)
                for dt in range(2):
                    c2 = g * 2 + dt
                    pg_ = pgr.next()
                    P.dma("sp", pg_.t[:, 0:Ls], pgT[c2, :, t0:t0 + Ls], pg_, [], [pg_])
                    for blk in range(0, Ls, 512):
                        n = min(512, Ls - blk)
                        ps = pp.next()
                        P.mm(ps.t[:, 0:n], [(pwb.t[:, g, cc, dt * 128:(dt + 1) * 128], pls[cc].t[:, blk:blk + n])
                                            for cc in range(2)], [pwb] + pls, [ps])
                        st = pst.next()
                        P.stt(st.t[:, 0:n], ps.t[:, 0:n], psc.t[:, l * 8 + c2:l * 8 + c2 + 1], pg_.t[:, blk:blk + n],
                              ALU.mult, ALU.mult, [ps, psc, pg_], [st])
                        P.dma("sp", mT[c2, :, t0 + blk:t0 + blk + n], st.t[:, 0:n], st, [st], [])


def make_pool_tasks(nc, P, l, with_ctx, G, ph, pB):
    pool_w, pvT, pgT, mT, edge, psc = G["pool_w"], G["pvT"], G["pgT"], G["mT"], G["edge"], G["psc"]
    SEG = 1024
    WP = SEG + 16
    pwb = Buf("pwb", ph.enter_context(SBT(nc, "pwb", [128, 4, 2, 256], BF16)), P.new_dsem())
    vb = Ring(P, ph, "vb", 2, [128, WP], BF16, dma=True)
    vp = Ring(P, ph, "vp", 2, [128, WP], F32)
    sr = Ring(P, ph, "sr", 3, [128, WP], F32)
    plr = Ring(P, ph, "plr", 4, [128, SEG], BF16)
    pgr = Ring(P, ph, "pgr", 2, [128, SEG], BF16, dma=True)
    pst = Ring(P, ph, "pst", 2, [128, 512], BF16, dma=True)
    etr = Ring(P, ph, "etr", 2, [128, 16], F32)
    pwsrc = pool_w[l].rearrange("g (c p) d -> p g c d", p=128)
    P.dma_multi("pool", [(pwb.t[:, g_], pwsrc[:, g_]) for g_ in range(4)], pwb, [], [pwb])
    seqs = ([(0, C)] if with_ctx else []) + [(C, L)]
    units = []
    for g in range(4):
        for (q0, Lq) in seqs:
            for s0 in range(q0, q0 + Lq, SEG):
                units.append((g, q0, Lq, s0, min(SEG, q0 + Lq - s0)))

    def dve_part(g, q0, Lq, s0, Ls, out):
        w = WINS[g]
        Wd = Ls + 16
        base = s0 - 8
        lo = max(q0, base)
        hi = min(q0 + Lq, s0 + Ls + 8)
        for cc in range(2):
            c = g * 2 + cc
            v = vb.next()
            P.dma("sp", v.t[:, lo - base:hi - base], pvT[c, :, lo:hi], v, [], [v])
            vp_ = vp.next()
            if lo > base:
                P.memset(vp_.t[:, 0:lo - base], 0.0, [], [vp_])
            if hi < base + Wd:
                P.memset(vp_.t[:, hi - base:Wd], 0.0, [], [vp_])
            P.copy(vp_.t[:, lo - base:hi - base], v.t[:, lo - base:hi - base], [v, vp_], [vp_])
            cur = vp_
            ln = Wd
            for s_ in range(g + 1):
                sh = 1 << s_
                nx = sr.next()
                ln = ln - sh
                P.tt(nx.t[:, 0:ln], cur.t[:, 0:ln], cur.t[:, sh:sh + ln], ALU.add, [cur], [nx])
                cur = nx
            off = 8 - w // 2
            pl_ = plr.next()
            P.stt(pl_.t[:, 0:Ls], cur.t[:, off:off + Ls], 1.0 / w, vp_.t[:, 8:8 + Ls], ALU.mult, ALU.subtract,
                  [cur, vp_], [pl_])
            if s0 == q0 or s0 + Ls == q0 + Lq:
                et = etr.next()
                if s0 == q0:
                    P.tt(et.t[:, 0:8], cur.t[:, off:off + 8], edge.t[:, g * 16:g * 16 + 8], ALU.mult,
                         [cur, edge, et], [et])
                    P.tt(pl_.t[:, 0:8], et.t[:, 0:8], vp_.t[:, 8:16], ALU.subtract, [et, vp_, pl_], [pl_])
                if s0 + Ls == q0 + Lq:
                    P.tt(et.t[:, 8:16], cur.t[:, off + Ls - 8:off + Ls], edge.t[:, g * 16 + 8:g * 16 + 16],
                         ALU.mult, [cur, edge, et], [et])
                    P.tt(pl_.t[:, Ls - 8:Ls], et.t[:, 8:16], vp_.t[:, Ls:Ls + 8], ALU.subtract,
                         [et, vp_, pl_], [pl_])
            out.append(pl_)

    def pe_part(g, q0, Lq, s0, Ls, pls):
        for dt in range(2):
            c2 = g * 2 + dt
            pg_ = pgr.next()
            P.dma("sp", pg_.t[:, 0:Ls], pgT[c2, :, s0:s0 + Ls], pg_, [], [pg_])
            for blk in range(0, Ls, 512):
                n = min(512, Ls - blk)
                P.mm(pB.t[:, 0:n], [(pwb.t[:, g, cc, dt * 128:(dt + 1) * 128], pls[cc].t[:, blk:blk + n])
                                    for cc in range(2)], [pwb] + pls, [pB])
                st = pst.next()
                P.stt(st.t[:, 0:n], pB.t[:, 0:n], psc.t[:, l * 8 + c2:l * 8 + c2 + 1], pg_.t[:, blk:blk + n],
                      ALU.mult, ALU.mult, [pB, psc, pg_], [st])
                P.dma("sp", mT[c2, :, s0 + blk:s0 + blk + n], st.t[:, 0:n], st, [st], [])

    results = {}
    tasks = []
    for i, u in enumerate(units):
        def task(i=i, u=u):
            if i > 0:
                pe_part(*units[i - 1], results[i - 1])
            outl = []
            dve_part(*u, outl)
            results[i] = outl
        tasks.append(task)

    def last(n=len(units)):
        pe_part(*units[n - 1], results[n - 1])
    tasks.append(last)
    return tasks


def phase_attn(nc, P, l, with_ctx, G):
    qT, qcT, kT, Vs, agT, mT = G["qT"], G["qcT"], G["kT"], G["Vs"], G["agT"], G["mT"]
    onesb, neglam, sws, selb = G["onesb"], G["neglam"], G["sws"], G["selb"]
    with ExitStack() as ph:
        kTh = Ring(P, ph, "kTh", 2, [128, LT], BF16, dma=True)
        Vh = Ring(P, ph, "Vh", 2, [128, NT * 128], BF16, dma=True)
        qTh = Ring(P, ph, "qTh", 2, [128, L], BF16, dma=True)
        agh = Ring(P, ph, "agh", 2, [128, LT], BF16, dma=True)
        qch = Ring(P, ph, "qch", 2, [128, C], BF16, dma=True)
        Eb = Ring(P, ph, "Eb", 3, [128, 1024], BF16)
        orA = Ring(P, ph, "orA", 2, [128, 512], F32)
        orB = Ring(P, ph, "orB", 2, [128, 512], F32)
        zraw = Ring(P, ph, "zraw", 2, [64, 512], F32)
        rzf = Ring(P, ph, "rzf", 2, [64, 512], F32)
        rzh = Ring(P, ph, "rzh", 2, [64, 512], BF16)
        rzd = Ring(P, ph, "rzd", 2, [64, 512], F32)
        rzl = Ring(P, ph, "rzl", 2, [64, 512], BF16)
        t0r = Ring(P, ph, "t0r", 2, [128, 512], F32)
        t1r = Ring(P, ph, "t1r", 2, [128, 512], F32)
        ob = Ring(P, ph, "ob", 2, [128, 512], F32)
        sqb = Ring(P, ph, "sqb", 2, [128, 512], BF16)
        rsb = Ring(P, ph, "rsb", 2, [128, 512], F32)
        ub = Ring(P, ph, "ub", 2, [128, 512], F32)
        ost = Ring(P, ph, "ost", 2, [128, 512], BF16, dma=True)
        pS = Ring(P, ph, "pS", 2, [128, 1024], F32, psum=True)
        pO = Ring(P, ph, "pO", 1, [128, 1024], F32, psum=True).next()
        pZ = Ring(P, ph, "pZ", 1, [128, 512], F32, psum=True).next()
        pB = Ring(P, ph, "pB", 1, [128, 512], F32, psum=True).next()

        def v2(ap, N):
            return ap.rearrange("p (m n) -> p m n", m=2)[:, :, 0:N]

        pool_tasks = make_pool_tasks(nc, P, l, with_ctx, G, ph, pB)
        blk_count = [0]

        stages = []

        def run_stages(upto):
            while stages and stages[0][0] <= upto:
                stages.pop(0)[1]()

        for h in range(H):
            k_ = kTh.next()
            P.dma("sp", k_.t[:], kT[h], k_, [], [k_])
            q_ = qTh.next()
            P.dma("sp", q_.t[:], qT[h], q_, [], [q_])
            v_ = Vh.next()
            P.dma("sp", v_.t[:], Vs[h].rearrange("p t e -> p (t e)"), v_, [], [v_])
            a_ = agh.next()
            P.dma("sp", a_.t[:], agT[h], a_, [], [a_])
            if with_ctx:
                qc_ = qch.next()
                P.dma("sp", qc_.t[:], qcT[h], qc_, [], [qc_])
            blocks = []
            if with_ctx:
                blocks.append((qc_, qc_.t[:, 0:C], 0, C, [0, 1]))
            for qb in range(8):
                blocks.append((q_, q_.t[:, qb * 512:(qb + 1) * 512], C + qb * 512, 512, list(range(NT))))
            for (qbuf, qsrc, tok0, N, kts) in blocks:
                nk = len(kts)
                if nk < 10:
                    run_stages(10 ** 9)

                def S_op(kt):
                    ps = pS.next()
                    P.mm_multi([(ps.t[:, 0:N], k_.t[0:64, kt * 128:(kt + 1) * 128], qsrc[0:64, :], True, True, None),
                                (ps.t[:, 512:512 + N], k_.t[64:128, kt * 128:(kt + 1) * 128], qsrc[64:128, :],
                                 True, True, None)], [k_, qbuf], [ps])
                    return ps

                ps_list = {0: S_op(kts[0])}
                if nk > 1:
                    ps_list[1] = S_op(kts[1])
                for i, kt in enumerate(kts):
                    ps = ps_list.pop(i)
                    e_ = Eb.next()
                    P.act(v2(e_.t[:], N), v2(ps.t[:], N), AF.Exp, [ps], [e_], scale=0.125)
                    run_stages(i)
                    if i + 2 < nk:
                        ps_list[i + 2] = S_op(kts[i + 2])
                    first = (i == 0)
                    last = (i == nk - 1)
                    vt = v_.t[:, kt * 128:(kt + 1) * 128]
                    items = [(pO.t[:, 0:N], vt, e_.t[:, 0:N], first, last, None),
                             (pO.t[:, 512:512 + N], vt, e_.t[:, 512:512 + N], first, last, None),
                             (pZ.t[0:32, 0:N], onesb.t[:, 0:32], e_.t[:, 0:N], first, last, (0, 0)),
                             (pZ.t[32:64, 0:N], onesb.t[:, 0:32], e_.t[:, 512:512 + N], first, last, (0, 32))]
                    P.mm_multi(items, [e_, v_, onesb], [pO, pZ] if (first or last) else [])
                run_stages(10 ** 9)
                oa = orA.next()
                P.act(oa.t[:, 0:N], pO.t[:, 0:N], AF.Copy, [pO], [oa])
                obb = orB.next()
                P.copy(obb.t[:, 0:N], pO.t[:, 512:512 + N], [pO], [obb])
                zr = zraw.next()
                P.act(zr.t[0:64, 0:N], pZ.t[0:64, 0:N], AF.Copy, [pZ], [zr])

                def stageA(oa=oa, zr=zr, N=N):
                    rf = rzf.next()
                    P.recip(rf.t[0:64, 0:N], zr.t[0:64, 0:N], [zr], [rf])
                    rh = rzh.next()
                    P.copy(rh.t[0:64, 0:N], rf.t[0:64, 0:N], [rf], [rh])
                    rd = rzd.next()
                    P.tt(rd.t[0:64, 0:N], rf.t[0:64, 0:N], rh.t[0:64, 0:N], ALU.subtract, [rf, rh], [rd])
                    rl = rzl.next()
                    P.copy(rl.t[0:64, 0:N], rd.t[0:64, 0:N], [rd], [rl])
                    P.mm(pB.t[:, 0:N], [(selb.t[0:64, 0:128], rh.t[0:64, 0:N]), (selb.t[0:64, 0:128], rl.t[0:64, 0:N])],
                         [selb, rh, rl], [pB])
                    t0 = t0r.next()
                    P.tt(t0.t[:, 0:N], oa.t[:, 0:N], pB.t[:, 0:N], ALU.mult, [oa, pB], [t0])
                    return rh, rl, t0

                def stageB(obb=obb, N=N, carry={}):
                    rh, rl, t0 = carry["a"]
                    P.mm(pB.t[:, 0:N], [(selb.t[0:64, 128:256], rh.t[0:64, 0:N]),
                                        (selb.t[0:64, 128:256], rl.t[0:64, 0:N])], [selb, rh, rl], [pB])
                    t1 = t1r.next()
                    P.tt(t1.t[:, 0:N], obb.t[:, 0:N], pB.t[:, 0:N], ALU.mult, [obb, pB], [t1])
                    o_ = ob.next()
                    P.stt(o_.t[:, 0:N], t1.t[:, 0:N], neglam.t[:, l:l + 1], t0.t[:, 0:N], ALU.mult, ALU.add,
                          [t1, t0, neglam], [o_])
                    sq_ = sqb.next()
                    P.tt(sq_.t[:, 0:N], o_.t[:, 0:N], o_.t[:, 0:N], ALU.mult, [o_], [sq_])
                    return o_, sq_

                def stageC(N=N, tok0=tok0, a_=a_, h=h, carry={}):
                    o_, sq_ = carry["b"]
                    P.mm(pB.t[:, 0:N], [(onesb.t[:], sq_.t[:, 0:N])], [sq_, onesb], [pB])
                    rs_ = rsb.next()
                    P.act(rs_.t[:, 0:N], pB.t[:, 0:N], AF.Ln, [pB], [rs_], scale=1.0 / 128, bias=EPS)
                    P.act(rs_.t[:, 0:N], rs_.t[:, 0:N], AF.Exp, [rs_], [rs_], scale=-0.5)
                    u_ = ub.next()
                    P.stt(u_.t[:, 0:N], o_.t[:, 0:N], sws.t[:, l:l + 1], rs_.t[:, 0:N], ALU.mult, ALU.mult,
                          [o_, sws, rs_], [u_])
                    st = ost.next()
                    P.tt(st.t[:, 0:N], u_.t[:, 0:N], a_.t[:, tok0:tok0 + N], ALU.mult, [u_, a_], [st])
                    P.dma("sp", mT[8 + h, :, tok0:tok0 + N], st.t[:, 0:N], st, [st], [])

                carry = {}

                def mkA(f=stageA, carry=carry):
                    carry["a"] = f()

                def mkB(f=stageB, carry=carry):
                    carry["b"] = f(carry=carry)

                def mkC(f=stageC, carry=carry):
                    f(carry=carry)

                stages.extend([(3, mkA), (5, mkB), (8, mkC)])
                blk_count[0] += 1
                if blk_count[0] % 3 == 0 and pool_tasks:
                    pool_tasks.pop(0)()
        run_stages(10 ** 9)
        while pool_tasks:
            pool_tasks.pop(0)()


def phase_out(nc, P, l, x_src, c_src, with_ctx, G):
    modd, w_out, mT, y, ctx1 = G["modd"], G["w_out"], G["mT"], G["y"], G["ctx1"]
    with ExitStack() as ph:
        S = ph.enter_context
        wo = Buf("wo", S(SBT(nc, "wo", [128, KC, D], BF16)), P.new_dsem())
        gx = Buf("gx", S(SBT(nc, "gx", [128, D], F32)), P.new_dsem())
        gc = Buf("gc", S(SBT(nc, "gc", [128, D], F32)), P.new_dsem())
        mTb = Ring(P, ph, "mTb", 2, [128, KC, 512], BF16, dma=True)
        xring = Ring(P, ph, "xo", 2, [128, D], F32, dma=True)
        tr = Ring(P, ph, "tr", 2, [128, 512], F32)
        ot = Ring(P, ph, "ot", 2, [128, D], F32, dma=True)
        po = Ring(P, ph, "po", 4, [128, 512], F32, psum=True)
        wosrc = G["w_out_bf"][l].rearrange("(k p) n -> p k n", p=128)
        P.dma_multi("pool", [(wo.t[:, 2 * j:2 * j + 2, :], wosrc[:, 2 * j:2 * j + 2, :]) for j in range(8)],
                    wo, [G["wbf_out"][l]], [wo])
        P.dma("sp", gx.t[:], modd[l, 0, 2 * D:3 * D].partition_broadcast(128), gx, [], [gx])
        P.dma("sp", gc.t[:], modd[l, 1, 2 * D:3 * D].partition_broadcast(128), gc, [], [gc])
        blks = blocks_of_tiles(True)
        if not with_ctx:
            blks = blks[1:]
        for tiles in blks:
            N = 128 * len(tiles)
            tok0 = tiles[0] * 128
            mb = mTb.next()
            msrc = mT[:, :, tok0:tok0 + N].rearrange("c p t -> p c t")
            P.dma_multi("sp", [(mb.t[:, 4 * j:4 * j + 4, 0:N], msrc[:, 4 * j:4 * j + 4, :]) for j in range(4)],
                        mb, [], [mb])
            for pos, T in enumerate(tiles):
                xt = xring.next()
                if T < 2:
                    src = c_src[T * 128:(T + 1) * 128, :]
                    dst = ctx1[T * 128:(T + 1) * 128, :]
                    g_ = gc
                else:
                    src = x_src[(T - 2) * 128:(T - 1) * 128, :]
                    dst = y[(T - 2) * 128:(T - 1) * 128, :]
                    g_ = gx
                P.dma("sp", xt.t[:], src, xt, [], [xt])
                ot_ = ot.next()
                for n in range(4):
                    ps = po.next()
                    P.mm(ps.t[:, :], [(mb.t[:, k, pos * 128:(pos + 1) * 128], wo.t[:, k, n * 512:(n + 1) * 512])
                                      for k in range(KC)], [mb, wo], [ps])
                    t_ = tr.next()
                    P.tt(t_.t[:], ps.t[:], g_.t[:, n * 512:(n + 1) * 512], ALU.mult, [ps, g_], [t_])
                    P.tt(ot_.t[:, n * 512:(n + 1) * 512], t_.t[:], xt.t[:, n * 512:(n + 1) * 512], ALU.add,
                         [t_, xt, ot_], [ot_])
                P.dma("sp", dst, ot_.t[:], ot_, [ot_], [])


def _consts():
    ident = np.eye(128, dtype=np.float32)
    inv = (np.float32(10000.0) ** (-np.arange(16, dtype=np.float32) / np.float32(16))).astype(np.float32)
    cos = np.ones((128, NT, 32), np.float32)
    sin = np.zeros((128, NT, 32), np.float32)
    t = (np.arange(NT - 2)[None, :] * 128 + np.arange(128)[:, None])
    row = (t // 64).astype(np.float32)
    col = (t % 64).astype(np.float32)
    ar = row[:, :, None] * inv
    ac = col[:, :, None] * inv
    cos[:, 2:, 0:16] = np.cos(ar)
    cos[:, 2:, 16:32] = np.cos(ac)
    sin[:, 2:, 0:16] = np.sin(ar)
    sin[:, 2:, 16:32] = np.sin(ac)
    edge = np.zeros((4, 16), np.float32)
    for g, w in enumerate(WINS):
        for i in range(8):
            edge[g, i] = 1.0 / min(w, i + w // 2)
            edge[g, 8 + i] = 1.0 / min(w, (8 - i) + w // 2)
    return ident, cos.reshape(128, NT * 32), sin.reshape(128, NT * 32), edge.reshape(64)


def make_in_maps(x, c, ctx, c_ctx, norm_w, w_ada, b_ada, w_in, pool_w, pool_scale, q_norm_w, k_norm_w,
                 lambda_q1, lambda_k1, lambda_q2, lambda_k2, subln_w, w_out):
    f = lambda a: np.ascontiguousarray(np.asarray(a, dtype=np.float32))
    ident, cos, sin, edge = _consts()
    pscT = f(np.asarray(pool_scale).reshape(NLAYERS, 8, 128).transpose(2, 0, 1).reshape(128, 16))
    swT = f(np.asarray(subln_w).T)
    lams = f(np.stack([np.asarray(lambda_q1), np.asarray(lambda_k1), np.asarray(lambda_q2),
                       np.asarray(lambda_k2)], axis=1).reshape(-1))
    shared = {
        "norm_w": f(norm_w), "w_ada": f(w_ada), "b_ada": f(b_ada), "w_in": f(w_in), "pool_w": f(pool_w),
        "pscT": pscT, "qw": f(np.asarray(q_norm_w).reshape(-1)), "kw": f(np.asarray(k_norm_w).reshape(-1)),
        "lams": lams, "swT": swT, "w_out": f(w_out), "ident": ident, "rope_cos": cos, "rope_sin": sin,
        "edge_rc": edge,
    }
    x = np.asarray(x)
    c = np.asarray(c)
    ctx = np.asarray(ctx)
    c_ctx = np.asarray(c_ctx)
    maps = []
    for b in range(x.shape[0]):
        cond = np.stack([c[b], c_ctx], axis=-1).reshape(KC, 128, 2).transpose(1, 0, 2).reshape(128, 32)
        m = dict(shared)
        m["x"] = f(x[b])
        m["ctx"] = f(ctx[b])
        m["cond"] = f(cond)
        maps.append(m)
    return maps


def kernel(**inputs):
    maps = make_in_maps(**inputs)
    nc = build()
    res = run_bass_kernel_spmd(nc, maps, core_ids=list(range(len(maps))))
    return np.stack([np.asarray(r["y"], dtype=np.float32) for r in res.results], axis=0)
```

```python
import math
from contextlib import ExitStack

import numpy as np

import concourse.bass as bass
import concourse.mybir as mybir
from concourse.bass_utils import run_bass_kernel_spmd

F32 = mybir.dt.float32
BF16 = mybir.dt.bfloat16
ALU = mybir.AluOpType
AF = mybir.ActivationFunctionType
AX = mybir.AxisListType

D = 2048
KC = 16
L = 4096
C = 256
LT = L + C
NT = LT // 128
H = 8
DIN = 6144
EPS = 1e-6
WINS = (2, 4, 8, 16)
NLAYERS = 2

DEBUG_SCRATCH = False
STOP_AFTER = None
N_DSEMS = 60


def lam_init_of(l):
    return 0.8 - 0.6 * math.exp(-0.3 * l)


_UNIQ = [0]


def SBT(nc, name, shape, dt):
    _UNIQ[0] += 1
    return nc.sbuf_tensor("%s_s%d" % (name, _UNIQ[0]), shape, dt)


def PST(nc, name, shape, dt):
    _UNIQ[0] += 1
    return nc.psum_tensor("%s_p%d" % (name, _UNIQ[0]), shape, dt)


class Sem:
    def __init__(self, h):
        self.h = h
        self.val = 0


class Buf:
    def __init__(self, name, t=None, dsem=None):
        self.name = name
        self.t = t
        self.dsem = dsem
        self.w = {}
        self.r = {}


class Queue:
    def __init__(self, name):
        self.name = name
        self.sem = None
        self.ops = []
        self.seen = {}


def _merge(dst, src):
    for s, v in src.items():
        if dst.get(s, 0) < v:
            dst[s] = v


class Prog:
    def __init__(self, nc, es):
        self.nc = nc
        self.qs = {n: Queue(n) for n in ("pe", "act", "dve", "pool", "sp")}
        for n in ("pe", "act", "dve"):
            self.qs[n].sem = Sem(es.enter_context(nc.semaphore("c_" + n)))
        self.dsems = [Sem(es.enter_context(nc.semaphore("d%d" % i))) for i in range(N_DSEMS)]
        self.dnext = 0
        self.nops = 0

    def new_dsem(self):
        s = self.dsems[self.dnext]
        self.dnext += 1
        return s

    def phase_reset(self):
        self.dnext = 0

    def op(self, qn, fn, R=(), W=(), dsem=None):
        q = self.qs[qn]
        need = {}
        for b in R:
            _merge(need, b.w)
        for b in W:
            _merge(need, b.w)
            _merge(need, b.r)
        waits = []
        for s, v in need.items():
            if q.seen.get(s, 0) < v:
                waits.append((s, v))
                q.seen[s] = v
        if dsem is not None:
            sem = dsem
            sem.val += 16
            inc = 16
        else:
            sem = q.sem
            sem.val += 1
            inc = 1
        val = sem.val
        for b in R:
            if b.r.get(sem, 0) < val:
                b.r[sem] = val
        for b in W:
            b.w = {sem: val}
            b.r = {}
        q.ops.append((waits, fn, sem, inc))
        self.nops += 1

    def barrier(self):
        sems = [self.qs[n].sem for n in ("pe", "act", "dve")] + self.dsems
        for q in self.qs.values():
            waits = []
            for s in sems:
                if s.val > q.seen.get(s, 0):
                    waits.append((s, s.val))
                    q.seen[s] = s.val
            q.ops.append((waits, None, None, 0))

    def emit(self):
        nc = self.nc
        with nc.Block() as block:
            for name, deco in (("sp", block.sync), ("act", block.scalar), ("dve", block.vector),
                               ("pool", block.gpsimd), ("pe", block.tensor)):
                q = self.qs[name]
                ops = q.ops
                q.ops = []

                def body(eng, ops=ops):
                    for waits, fn, sem, inc in ops:
                        for s, v in waits:
                            eng.wait_ge(s.h, v)
                        if fn is not None:
                            ins = fn(eng)
                            ins.then_inc(sem.h, inc)

                deco(body)

    def tt(self, out, in0, in1, op, R, W, q="dve"):
        self.op(q, lambda e: e.tensor_tensor(out=out, in0=in0, in1=in1, op=op), R, W)

    def ts(self, out, in0, s1, s2, op0, op1, R, W, q="dve"):
        self.op(q, lambda e: e.tensor_scalar(out=out, in0=in0, scalar1=s1, scalar2=s2, op0=op0, op1=op1), R, W)

    def stt(self, out, in0, scalar, in1, op0, op1, R, W, q="dve"):
        self.op(q, lambda e: e.scalar_tensor_tensor(out=out, in0=in0, scalar=scalar, in1=in1, op0=op0, op1=op1), R, W)

    def copy(self, out, in_, R, W, q="dve"):
        self.op(q, lambda e: e.tensor_copy(out=out, in_=in_), R, W)

    def red(self, out, in_, R, W, q="dve"):
        self.op(q, lambda e: e.tensor_reduce(out=out, in_=in_, axis=AX.X, op=ALU.add), R, W)

    def recip(self, out, in_, R, W):
        self.op("dve", lambda e: e.reciprocal(out=out, in_=in_), R, W)

    def memset(self, ap, val, R, W, q="dve"):
        self.op(q, lambda e: e.memset(ap, val), R, W)

    def act(self, out, in_, func, R, W, scale=1.0, bias=0.0, accum=None):
        if accum is None:
            self.op("act", lambda e: e.activation(out=out, in_=in_, func=func, scale=scale, bias=bias), R, W)
        else:
            self.op("act", lambda e: e.activation(out=out, in_=in_, func=func, scale=scale, bias=bias,
                                                  accum_out=accum), R, W)

    def dma(self, q, out, in_, buf, R, W):
        self.op(q, lambda e: e.dma_start(out=out, in_=in_), R, W, dsem=buf.dsem)

    def dma_multi(self, q, pairs, buf, R, W):
        sem = buf.dsem
        n = len(pairs)

        def fn(e):
            ins = None
            for i, (o, a) in enumerate(pairs):
                ins = e.dma_start(out=o, in_=a)
                if i < n - 1:
                    ins.then_inc(sem.h, 16)
            return ins
        sem.val += 16 * (n - 1)
        self.op(q, fn, R, W, dsem=sem)

    def mm(self, out, pairs, R, W, start=True, stop=True):
        def fn(e):
            n = len(pairs)
            ins = None
            for i, (a, b) in enumerate(pairs):
                ins = e.matmul(out, lhsT=a, rhs=b, start=(start and i == 0), stop=(stop and i == n - 1))
            return ins
        self.op("pe", fn, R, W)

    def mm_multi(self, items, R, W):
        def fn(e):
            ins = None
            for (o, a, b, st, sp, tp) in items:
                if tp is None:
                    ins = e.matmul(o, lhsT=a, rhs=b, start=st, stop=sp)
                else:
                    ins = e.matmul(o, lhsT=a, rhs=b, start=st, stop=sp, tile_position=tp)
            return ins
        self.op("pe", fn, R, W)

    def transposes(self, items, ident, R, W):
        def fn(e):
            ins = None
            for (o, i) in items:
                ins = e.transpose(out=o, in_=i, identity=ident)
            return ins
        self.op("pe", fn, R, W)


class Ring:
    def __init__(self, prog, es, name, n, shape, dtype, dma=False, psum=False):
        nc = prog.nc
        self.slots = []
        for i in range(n):
            if psum:
                t = es.enter_context(PST(nc, "%s%d" % (name, i), shape, dtype))
            else:
                t = es.enter_context(SBT(nc, "%s%d" % (name, i), shape, dtype))
            self.slots.append(Buf("%s%d" % (name, i), t, prog.new_dsem() if dma else None))
        self.i = 0
        self.n = n

    def next(self):
        b = self.slots[self.i % self.n]
        self.i += 1
        return b


def bc(ap, shape):
    return ap.to_broadcast(shape)


def build():
    nc = bass.Bass("TRN2", target_bir_lowering=False)
    skind = "ExternalOutput" if DEBUG_SCRATCH else "Internal"

    def din(name, shape, dt=F32):
        return nc.dram_tensor(name, shape, dt, kind="ExternalInput").ap()

    def dscr(name, shape, dt):
        return nc.dram_tensor(name, shape, dt, kind=skind).ap()

    x_in = din("x", [L, D])
    ctx_in = din("ctx", [C, D])
    cond_in = din("cond", [128, 32])
    norm_w = din("norm_w", [NLAYERS, D])
    w_ada = din("w_ada", [NLAYERS, D, DIN])
    b_ada = din("b_ada", [NLAYERS, DIN])
    w_in = din("w_in", [NLAYERS, D, DIN])
    pool_w = din("pool_w", [NLAYERS, 4, 256, 256])
    psc_in = din("pscT", [128, 16])
    qw_in = din("qw", [NLAYERS * 64])
    kw_in = din("kw", [NLAYERS * 64])
    lams_in = din("lams", [NLAYERS * 4 * 64])
    sw_in = din("swT", [128, NLAYERS])
    w_out = din("w_out", [NLAYERS, D, D])
    ident_in = din("ident", [128, 128])
    cos_in = din("rope_cos", [128, NT * 32])
    sin_in = din("rope_sin", [128, NT * 32])
    edge_in = din("edge_rc", [64])

    y = nc.dram_tensor("y", [L, D], F32, kind="ExternalOutput").ap()

    modd = dscr("modd", [NLAYERS, 2, DIN], F32)
    qT = dscr("qT", [H, 128, L], BF16)
    qcT = dscr("qcT", [H, 128, C], BF16)
    kT = dscr("kT", [H, 128, LT], BF16)
    Vs = dscr("Vs", [H, 128, NT, 128], BF16)
    pvT = dscr("pvT", [8, 128, LT], BF16)
    pgT = dscr("pgT", [8, 128, LT], BF16)
    agT = dscr("agT", [8, 128, LT], BF16)
    mT = dscr("mT", [16, 128, LT], BF16)
    ctx1 = dscr("ctx1", [C, D], F32)
    w_in_bf = nc.dram_tensor("w_in_bf", [NLAYERS, D, DIN], BF16, kind="Internal").ap()
    w_out_bf = nc.dram_tensor("w_out_bf", [NLAYERS, D, D], BF16, kind="Internal").ap()

    with ExitStack() as top:
        P = Prog(nc, top)
        E = top.enter_context

        def pers(name, shape, dt, dma=False):
            t = E(SBT(nc, name, shape, dt))
            return Buf(name, t, P.new_dsem() if dma else None)

        identf = pers("identf", [128, 128], F32, dma=True)
        identb = pers("identb", [128, 128], BF16)
        onesb = pers("onesb", [128, 128], BF16)
        cosT = pers("cosT", [128, NT * 32], F32, dma=True)
        sinT = pers("sinT", [128, NT * 32], F32, dma=True)
        edge = pers("edge", [128, 64], F32, dma=True)
        psc = pers("psc", [128, 16], F32, dma=True)
        sws = pers("sws", [128, NLAYERS], F32, dma=True)
        qw = pers("qw", [128, NLAYERS * 64], F32, dma=True)
        kw = pers("kw", [128, NLAYERS * 64], F32, dma=True)
        lamt = pers("lamt", [128, NLAYERS * 4 * 64], F32, dma=True)
        lpr = pers("lpr", [128, 4 * 64], F32)
        ls4 = pers("ls4", [128, 4], F32)
        le4 = pers("le4", [128, 4], F32)
        nl0 = pers("nl0", [128, NLAYERS], F32)
        neglam = pers("neglam", [128, NLAYERS], F32)
        wbf_in = [Buf("wbf_in%d" % l_, None, P.new_dsem()) for l_ in range(NLAYERS)]
        wbf_out = [Buf("wbf_out%d" % l_, None, P.new_dsem()) for l_ in range(NLAYERS)]
        n_pers_dsems = P.dnext

        def conv_in(l_):
            P.dma_multi("pool", [(w_in_bf[l_][r * 128:(r + 1) * 128, :], w_in[l_][r * 128:(r + 1) * 128, :])
                                 for r in range(KC)], wbf_in[l_], [], [wbf_in[l_]])

        def conv_out(l_):
            P.dma_multi("pool", [(w_out_bf[l_][r * 128:(r + 1) * 128, :], w_out[l_][r * 128:(r + 1) * 128, :])
                                 for r in range(KC)], wbf_out[l_], [], [wbf_out[l_]])


        P.dma("sp", identf.t[:], ident_in, identf, [], [identf])
        P.dma("sp", cosT.t[:], cos_in, cosT, [], [cosT])
        P.dma("sp", sinT.t[:], sin_in, sinT, [], [sinT])
        P.dma("sp", edge.t[:], edge_in.partition_broadcast(128), edge, [], [edge])
        P.dma("sp", psc.t[:], psc_in, psc, [], [psc])
        P.dma("sp", sws.t[:], sw_in, sws, [], [sws])
        P.dma("sp", qw.t[:], qw_in.partition_broadcast(128), qw, [], [qw])
        P.dma("sp", kw.t[:], kw_in.partition_broadcast(128), kw, [], [kw])
        P.dma("sp", lamt.t[:], lams_in.partition_broadcast(128), lamt, [], [lamt])
        selb = pers("selb", [64, 256], BF16)
        P.memset(selb.t[:], 0.0, [], [selb])
        P.memset(selb.t[0:1, 0:128], 1.0, [selb], [selb])
        P.memset(selb.t[32:33, 128:256], 1.0, [selb], [selb])
        P.copy(identb.t[:], identf.t[:], [identf], [identb])
        P.memset(onesb.t[:], 1.0, [], [onesb])
        for l in range(NLAYERS):
            P.ts(sws.t[:, l:l + 1], sws.t[:, l:l + 1], float(1.0 - lam_init_of(l)), 0.0, ALU.mult, ALU.add,
                 [sws], [sws])
        lv = lamt.t[:].rearrange("p (a j d) -> p a j d", j=2, d=64)
        P.tt(lpr.t[:].rearrange("p (a d) -> p a d", d=64), lv[:, :, 0, :], lv[:, :, 1, :], ALU.mult, [lamt], [lpr])
        P.red(ls4.t[:], lpr.t[:].rearrange("p (a d) -> p a d", d=64), [lpr], [ls4])
        P.act(le4.t[:], ls4.t[:], AF.Exp, [ls4], [le4])
        for l in range(NLAYERS):
            P.tt(nl0.t[:, l:l + 1], le4.t[:, 2 * l + 1:2 * l + 2], le4.t[:, 2 * l:2 * l + 1], ALU.subtract,
                 [le4], [nl0])
            P.ts(neglam.t[:, l:l + 1], nl0.t[:, l:l + 1], 1.0, float(-lam_init_of(l)), ALU.mult, ALU.add,
                 [nl0], [neglam])

        def end_phase():
            P.barrier()
            P.emit()
            P.dnext = n_pers_dsems

        with ExitStack() as ph:
            cs = Buf("cs", ph.enter_context(SBT(nc, "cs", [128, 32], F32)), P.new_dsem())
            e1 = Buf("e1", ph.enter_context(SBT(nc, "e1", [128, 32], F32)))
            scb = Buf("scb", ph.enter_context(SBT(nc, "scb", [128, 32], BF16)))
            bsb = Buf("bsb", ph.enter_context(SBT(nc, "bsb", [2, DIN], F32)), P.new_dsem())
            modsb = Buf("modsb", ph.enter_context(SBT(nc, "modsb", [2, DIN], F32)), P.new_dsem())
            wa = Ring(P, ph, "wa", 3, [128, KC, 512], BF16, dma=True)
            pmod = Ring(P, ph, "pmod", 2, [128, 512], F32, psum=True)
            P.dma("sp", cs.t[:], cond_in, cs, [], [cs])
            P.act(e1.t[:], cs.t[:], AF.Exp, [cs], [e1], scale=-1.0)
            P.act(e1.t[:], e1.t[:], AF.Ln, [e1], [e1], scale=1.0, bias=1.0)
            P.act(e1.t[:], e1.t[:], AF.Exp, [e1], [e1], scale=-1.0)
            P.tt(scb.t[:], cs.t[:], e1.t[:], ALU.mult, [cs, e1], [scb])
            for l in range(NLAYERS):
                P.dma("sp", bsb.t[:], b_ada[l].partition_broadcast(2), bsb, [], [bsb])
                for n in range(DIN // 512):
                    w = wa.next()
                    wsrc = w_ada[l][:, n * 512:(n + 1) * 512].rearrange("(k p) n -> p k n", p=128)
                    P.dma_multi("pool", [(w.t[:, 4 * j:4 * j + 4, :], wsrc[:, 4 * j:4 * j + 4, :]) for j in range(4)],
                                w, [], [w])
                    pm = pmod.next()
                    P.mm(pm.t[0:2, :], [(scb.t[:, 2 * k:2 * k + 2], w.t[:, k, :]) for k in range(KC)],
                         [scb, w], [pm])
                    P.tt(modsb.t[0:2, n * 512:(n + 1) * 512], pm.t[0:2, :], bsb.t[0:2, n * 512:(n + 1) * 512],
                         ALU.add, [pm, bsb], [modsb])
                P.dma("sp", modd[l], modsb.t[:], modsb, [modsb], [])
                if l == 0:
                    conv_in(0)
            end_phase()
        if STOP_AFTER == (0, "P0"):
            return nc

        for l in range(NLAYERS):
            x_src = x_in if l == 0 else y
            c_src = ctx_in if l == 0 else ctx1
            with_ctx = (l == 0)
            phase_A(nc, P, l, x_src, c_src, with_ctx, locals())
            end_phase()
            if STOP_AFTER == (l, "A"):
                return nc
            if l == 0:
                conv_out(0)
                for l2 in range(1, NLAYERS):
                    conv_in(l2)
                    conv_out(l2)
            phase_attn(nc, P, l, with_ctx, locals())
            end_phase()
            if STOP_AFTER == (l, "T"):
                return nc
            phase_out(nc, P, l, x_src, c_src, with_ctx, locals())
            end_phase()
            if STOP_AFTER == (l, "O"):
                return nc
    return nc


def blocks_of_tiles(with_ctx_full):
    blks = [[0, 1]]
    for j in range(8):
        blks.append([2 + 4 * j + i for i in range(4)])
    return blks


def phase_A(nc, P, l, x_src, c_src, with_ctx, G):
    modd, norm_w, w_in = G["modd"], G["norm_w"], G["w_in"]
    identb, cosT, sinT, qw, kw = G["identb"], G["cosT"], G["sinT"], G["qw"], G["kw"]
    pvT, pgT, agT, qT, qcT, kT, Vs = G["pvT"], G["pgT"], G["agT"], G["qT"], G["qcT"], G["kT"], G["Vs"]
    with ExitStack() as ph:
        S = ph.enter_context

        def sb(name, shape, dt, dma=False):
            return Buf(name, S(SBT(nc, name, shape, dt)), P.new_dsem() if dma else None)

        Gx = sb("Gx", [128, D], F32, True)
        Sx = sb("Sx", [128, D], F32, True)
        Gc = sb("Gc", [128, D], F32, True)
        Sc = sb("Sc", [128, D], F32, True)
        junk = S(SBT(nc, "junk", [128, D], BF16))
        xring = Ring(P, ph, "xr", 2, [128, D], F32, dma=True)
        ssq = Ring(P, ph, "ssq", 4, [128, 1], F32)
        rstd = Ring(P, ph, "rstd", 4, [128, 1], F32)
        tmp = Ring(P, ph, "tmp", 1, [128, D], F32, dma=True)
        hb = Ring(P, ph, "hb", 2, [128, D], BF16)
        pT = Ring(P, ph, "pT", 1, [128, D], BF16, psum=True)
        hT = [S(SBT(nc, "hT%d" % i, [128, KC, 512], BF16)) for i in range(2)]
        hTb = [[Buf("hT%d_%d" % (i, j)) for j in range(4)] for i in range(2)]
        wsl = Ring(P, ph, "wsl", 2, [128, KC, 512], BF16, dma=True)
        pmm = Ring(P, ph, "pmm", 4, [128, 512], F32, psum=True)
        pvst = Ring(P, ph, "pvst", 2, [128, 512], BF16, dma=True)
        gt = Ring(P, ph, "gt", 2, [128, 512], F32)
        gst = Ring(P, ph, "gst", 2, [128, 512], BF16, dma=True)
        vst = Ring(P, ph, "vst", 2, [128, 512], BF16, dma=True)
        sqr = Ring(P, ph, "sqr", 2, [128, 512], F32)
        ss8 = Ring(P, ph, "ss8", 2, [128, 8], F32)
        rs8 = Ring(P, ph, "rs8", 2, [128, 8], F32)
        xn = Ring(P, ph, "xn", 2, [128, 512], F32)
        rtmps = [Ring(P, ph, "rtmp%d" % a, 2, [128, 256], F32) for a in range(4)]
        qn = Ring(P, ph, "qn", 4, [128, 512], BF16)
        pq = Ring(P, ph, "pq", 1, [128, 512], BF16, psum=True)
        qst = Ring(P, ph, "qst", 2, [128, 512], BF16, dma=True)

        nwb = tmp.next()
        P.dma("sp", nwb.t[:], norm_w[l].partition_broadcast(128), nwb, [], [nwb])
        for (Gb, Sb_, r) in ((Gx, Sx, 0), (Gc, Sc, 1)):
            P.dma("sp", Sb_.t[:], modd[l, r, 0:D].partition_broadcast(128), Sb_, [], [Sb_])
            P.dma("sp", Gb.t[:], modd[l, r, D:2 * D].partition_broadcast(128), Gb, [], [Gb])
            P.stt(Gb.t[:], Gb.t[:], 1.0, nwb.t[:], ALU.add, ALU.mult, [Gb, nwb], [Gb])

        blks = blocks_of_tiles(True)

        def prep1(T):
            xt = xring.next()
            src = c_src[T * 128:(T + 1) * 128, :] if T < 2 else x_src[(T - 2) * 128:(T - 1) * 128, :]
            P.dma("sp", xt.t[:], src, xt, [], [xt])
            sq_ = ssq.next()
            P.memset(sq_.t[:], 0.0, [], [sq_])
            P.act(junk[:], xt.t[:], AF.Square, [xt, sq_], [sq_], accum=sq_.t[:])
            rs = rstd.next()
            P.act(rs.t[:], sq_.t[:], AF.Ln, [sq_], [rs], scale=1.0 / D, bias=EPS)
            P.act(rs.t[:], rs.t[:], AF.Exp, [rs], [rs], scale=-0.5)
            G_, S_ = (Gc, Sc) if T < 2 else (Gx, Sx)
            tm = tmp.next()
            P.stt(tm.t[:], xt.t[:], rs.t[:, 0:1], G_.t[:], ALU.mult, ALU.mult, [xt, rs, G_], [tm])
            hb_ = hb.next()
            P.tt(hb_.t[:], tm.t[:], S_.t[:], ALU.add, [tm, S_], [hb_])
            return hb_

        def prep2(hb_, slot, pos):
            pt = pT.next()
            P.transposes([(pt.t[:, k * 128:(k + 1) * 128], hb_.t[:, k * 128:(k + 1) * 128]) for k in range(KC)],
                         identb.t[:], [hb_, identb], [pt])
            P.act(hT[slot][:, :, pos * 128:(pos + 1) * 128], pt.t[:].rearrange("p (k t) -> p k t", t=128),
                  AF.Copy, [pt], [hTb[slot][pos]])

        deferred = []
        DEPTH = 2

        def drain(keep):
            while len(deferred) > keep:
                deferred.pop(0)()

        kinds = ["pv", "pv", "pg", "pg", "q", "q", "k", "k", "v", "v", "ag", "ag"]
        for pos, T in enumerate(blks[0]):
            prep2(prep1(T), 0, pos)
        for bi, tiles in enumerate(blks):
            slot = bi % 2
            N = 128 * len(tiles)
            tok0 = tiles[0] * 128
            is_ctx = (bi == 0)
            cbs = list(range(12)) if (with_ctx or not is_ctx) else [6, 7, 8, 9]
            nxt = blks[bi + 1] if bi + 1 < len(blks) else []
            events = {}
            hb_of = {}
            for i, Tn in enumerate(nxt):
                p1 = min(len(cbs) - 1, 2 * i)
                p2 = min(len(cbs) - 1, 2 * i + 2)
                events.setdefault(p1, []).append((float(i), lambda Tn=Tn, i=i: hb_of.__setitem__(i, prep1(Tn))))
                events.setdefault(p2, []).append((i + 1.5, lambda i=i, s_=1 - slot: prep2(hb_of[i], s_, i)))
            hbufs = hTb[slot][:len(tiles)]
            for ci, cb in enumerate(cbs):
                w = wsl.next()
                wsrc = G["w_in_bf"][l][:, cb * 512:(cb + 1) * 512].rearrange("(k p) n -> p k n", p=128)
                P.dma_multi("pool", [(w.t[:, 4 * j:4 * j + 4, :], wsrc[:, 4 * j:4 * j + 4, :]) for j in range(4)],
                            w, [G["wbf_in"][l]], [w])
                kind = kinds[cb]
                if kind in ("pv", "pg", "ag"):
                    for sub in range(4):
                        c = (cb % 2) * 4 + sub
                        drain(DEPTH)
                        ps = pmm.next()
                        P.mm(ps.t[:, 0:N], [(w.t[:, k, sub * 128:(sub + 1) * 128], hT[slot][:, k, 0:N])
                                            for k in range(KC)], [w] + hbufs, [ps])
                        if kind == "pv":
                            st = pvst.next()
                            P.act(st.t[:, 0:N], ps.t[:, 0:N], AF.Copy, [ps], [st])
                            P.dma("sp", pvT[c, :, tok0:tok0 + N], st.t[:, 0:N], st, [st], [])
                        else:
                            g1 = gt.next()
                            P.act(g1.t[:, 0:N], ps.t[:, 0:N], AF.Exp, [ps], [g1], scale=-1.0)
                            P.act(g1.t[:, 0:N], g1.t[:, 0:N], AF.Ln, [g1], [g1], scale=1.0, bias=1.0)
                            P.act(g1.t[:, 0:N], g1.t[:, 0:N], AF.Exp, [g1], [g1], scale=-1.0)
                            st = gst.next()
                            P.tt(st.t[:, 0:N], ps.t[:, 0:N], g1.t[:, 0:N], ALU.mult, [ps, g1], [st])
                            dst = pgT if kind == "pg" else agT
                            P.dma("sp", dst[c, :, tok0:tok0 + N], st.t[:, 0:N], st, [st], [])
                else:
                    h0 = (cb % 2) * 4
                    for pos, T in enumerate(tiles):
                        drain(DEPTH)
                        ps = pmm.next()
                        P.mm(ps.t[:, :], [(hT[slot][:, k, pos * 128:(pos + 1) * 128], w.t[:, k, :])
                                          for k in range(KC)], [w, hTb[slot][pos]], [ps])
                        if kind == "v":
                            st = vst.next()
                            P.act(st.t[:], ps.t[:], AF.Copy, [ps], [st])
                            P.dma("sp", Vs[h0:h0 + 4, :, T, :].rearrange("h p e -> p h e"),
                                  st.t[:].rearrange("p (h e) -> p h e", e=128), st, [st], [])
                            continue
                        wv = (qw if kind == "q" else kw)
                        sq_ = sqr.next()
                        P.act(sq_.t[:], ps.t[:], AF.Square, [ps], [sq_])
                        s8 = ss8.next()
                        P.red(s8.t[:], sq_.t[:].rearrange("p (g d) -> p g d", d=64), [sq_], [s8])
                        r8 = rs8.next()
                        P.act(r8.t[:], s8.t[:], AF.Ln, [s8], [r8], scale=1.0 / 64, bias=EPS)
                        P.act(r8.t[:], r8.t[:], AF.Exp, [r8], [r8], scale=-0.5)
                        xn_ = xn.next()
                        xg = xn_.t[:].rearrange("p (g d) -> p g d", d=64)
                        P.tt(xg, ps.t[:].rearrange("p (g d) -> p g d", d=64),
                             bc(r8.t[:].unsqueeze(2), [128, 8, 64]), ALU.mult, [ps, r8], [xn_])
                        P.tt(xg, xg, bc(wv.t[:, l * 64:(l + 1) * 64].unsqueeze(1), [128, 8, 64]), ALU.mult,
                             [xn_, wv], [xn_])
                        xv = xn_.t[:].rearrange("p (s h t i) -> p s h t i", s=8, h=2, t=2, i=16)
                        x1 = xv[:, :, :, 0, :]
                        x2 = xv[:, :, :, 1, :]
                        cs_ = bc(cosT.t[:, T * 32:(T + 1) * 32].rearrange("p (h i) -> p h i", i=16).unsqueeze(1),
                                 [128, 8, 2, 16])
                        sn_ = bc(sinT.t[:, T * 32:(T + 1) * 32].rearrange("p (h i) -> p h i", i=16).unsqueeze(1),
                                 [128, 8, 2, 16])
                        rts = [r_.next() for r_ in rtmps]
                        rv = [r_.t[:].rearrange("p (s h i) -> p s h i", s=8, h=2, i=16) for r_ in rts]
                        P.tt(rv[0], x1, cs_, ALU.mult, [xn_, cosT], [rts[0]])
                        P.tt(rv[1], x2, cs_, ALU.mult, [xn_, cosT], [rts[1]])
                        P.tt(rv[2], x2, sn_, ALU.mult, [xn_, sinT], [rts[2]])
                        P.tt(rv[3], x1, sn_, ALU.mult, [xn_, sinT], [rts[3]])
                        qn_ = qn.next()
                        qv = qn_.t[:].rearrange("p (s h t i) -> p s h t i", s=8, h=2, t=2, i=16)
                        P.tt(qv[:, :, :, 0, :], rv[0], rv[2], ALU.subtract, [rts[0], rts[2]], [qn_])
                        P.tt(qv[:, :, :, 1, :], rv[1], rv[3], ALU.add, [rts[1], rts[3]], [qn_])
                        if kind == "q":
                            if T < 2:
                                dst = qcT[h0:h0 + 4, :, T * 128:(T + 1) * 128]
                            else:
                                dst = qT[h0:h0 + 4, :, (T - 2) * 128:(T - 1) * 128]
                        else:
                            dst = kT[h0:h0 + 4, :, T * 128:(T + 1) * 128]

                        def stage(qn_=qn_, dst=dst):
                            pq_ = pq.next()
                            P.transposes([(pq_.t[:, hh * 128:(hh + 1) * 128], qn_.t[:, hh * 128:(hh + 1) * 128])
                                          for hh in range(4)], identb.t[:], [qn_, identb], [pq_])
                            st = qst.next()
                            P.act(st.t[:], pq_.t[:], AF.Copy, [pq_], [st])
                            P.dma("sp", dst.rearrange("h p t -> p h t"),
                                  st.t[:].rearrange("p (h t) -> p h t", t=128), st, [st], [])

                        deferred.append(stage)
                for _key, fn_ in sorted(events.get(ci, []), key=lambda e_: e_[0]):
                    fn_()
        drain(0)


def phase_pool(nc, P, l, with_ctx, G):
    pool_w, pvT, pgT, mT, edge, psc = G["pool_w"], G["pvT"], G["pgT"], G["mT"], G["edge"], G["psc"]
    with ExitStack() as ph:
        S = ph.enter_context
        WP = L + 16
        pwb = Buf("pwb", S(SBT(nc, "pwb", [128, 4, 2, 256], BF16)), P.new_dsem())
        vb = Ring(P, ph, "vb", 2, [128, L], BF16, dma=True)
        vp = Ring(P, ph, "vp", 2, [128, WP], F32)
        sr = Ring(P, ph, "sr", 3, [128, WP], F32)
        plr = Ring(P, ph, "plr", 4, [128, L], BF16)
        pgr = Ring(P, ph, "pgr", 2, [128, L], BF16, dma=True)
        pst = Ring(P, ph, "pst", 2, [128, 512], BF16, dma=True)
        etr = Ring(P, ph, "etr", 2, [128, 16], F32)
        pp = Ring(P, ph, "pp", 4, [128, 512], F32, psum=True)
        pwsrc = pool_w[l].rearrange("g (c p) d -> p g c d", p=128)
        P.dma_multi("pool", [(pwb.t[:, g_], pwsrc[:, g_]) for g_ in range(4)], pwb, [], [pwb])
        segs = ([(0, C)] if with_ctx else []) + [(C, L)]
        for g in range(4):
            w = WINS[g]
            for (t0, Ls) in segs:
                Wd = Ls + 16
                pls = []
                for cc in range(2):
                    c = g * 2 + cc
                    v = vb.next()
                    P.dma("sp", v.t[:, 0:Ls], pvT[c, :, t0:t0 + Ls], v, [], [v])
                    vp_ = vp.next()
                    P.memset(vp_.t[:, 0:8], 0.0, [], [vp_])
                    P.memset(vp_.t[:, 8 + Ls:16 + Ls], 0.0, [], [vp_])
                    P.copy(vp_.t[:, 8:8 + Ls], v.t[:, 0:Ls], [v, vp_], [vp_])
                    cur = vp_
                    ln = Wd
                    for s in range(g + 1):
                        sh = 1 << s
                        nx = sr.next()
                        ln = ln - sh
                        P.tt(nx.t[:, 0:ln], cur.t[:, 0:ln], cur.t[:, sh:sh + ln], ALU.add, [cur], [nx])
                        cur = nx
                    off = 8 - w // 2
                    pl_ = plr.next()
                    P.stt(pl_.t[:, 0:Ls], cur.t[:, off:off + Ls], 1.0 / w, vp_.t[:, 8:8 + Ls], ALU.mult, ALU.subtract,
                          [cur, vp_], [pl_])
                    et = etr.next()
                    P.tt(et.t[:, 0:8], cur.t[:, off:off + 8], edge.t[:, g * 16:g * 16 + 8], ALU.mult,
                         [cur, edge], [et])
                    P.tt(et.t[:, 8:16], cur.t[:, off + Ls - 8:off + Ls], edge.t[:, g * 16 + 8:g * 16 + 16], ALU.mult,
                         [cur, edge, et], [et])
                    P.tt(pl_.t[:, 0:8], et.t[:, 0:8], vp_.t[:, 8:16], ALU.subtract, [et, vp_, pl_], [pl_])
                    P.tt(pl_.t[:, Ls - 8:Ls], et.t[:, 8:16], vp_.t[:, Ls:Ls + 8], ALU.subtract, [et, vp_, pl_], [pl_])
                    pls.append(pl_)
                for dt in range(2):
                    c2 = g * 2 + dt
                    pg_ = pgr.next()
                    P.dma("sp", pg_.t[:, 0:Ls], pgT[c2, :, t0:t0 + Ls], pg_, [], [pg_])
                    for blk in range(0, Ls, 512):
                        n = min(512, Ls - blk)
                        ps = pp.next()
                        P.mm(ps.t[:, 0:n], [(pwb.t[:, g, cc, dt * 128:(dt + 1) * 128], pls[cc].t[:, blk:blk + n])
                                            for cc in range(2)], [pwb] + pls, [ps])
                        st = pst.next()
                        P.stt(st.t[:, 0:n], ps.t[:, 0:n], psc.t[:, l * 8 + c2:l * 8 + c2 + 1], pg_.t[:, blk:blk + n],
                              ALU.mult, ALU.mult, [ps, psc, pg_], [st])
                        P.dma("sp", mT[c2, :, t0 + blk:t0 + blk + n], st.t[:, 0:n], st, [st], [])


def make_pool_tasks(nc, P, l, with_ctx, G, ph, pB):
    pool_w, pvT, pgT, mT, edge, psc = G["pool_w"], G["pvT"], G["pgT"], G["mT"], G["edge"], G["psc"]
    SEG = 1024
    WP = SEG + 16
    pwb = Buf("pwb", ph.enter_context(SBT(nc, "pwb", [128, 4, 2, 256], BF16)), P.new_dsem())
    vb = Ring(P, ph, "vb", 2, [128, WP], BF16, dma=True)
    vp = Ring(P, ph, "vp", 2, [128, WP], F32)
    sr = Ring(P, ph, "sr", 3, [128, WP], F32)
    plr = Ring(P, ph, "plr", 4, [128, SEG], BF16)
    pgr = Ring(P, ph, "pgr", 2, [128, SEG], BF16, dma=True)
    pst = Ring(P, ph, "pst", 2, [128, 512], BF16, dma=True)
    etr = Ring(P, ph, "etr", 2, [128, 16], F32)
    pwsrc = pool_w[l].rearrange("g (c p) d -> p g c d", p=128)
    P.dma_multi("pool", [(pwb.t[:, g_], pwsrc[:, g_]) for g_ in range(4)], pwb, [], [pwb])
    seqs = ([(0, C)] if with_ctx else []) + [(C, L)]
    units = []
    for g in range(4):
        for (q0, Lq) in seqs:
            for s0 in range(q0, q0 + Lq, SEG):
                units.append((g, q0, Lq, s0, min(SEG, q0 + Lq - s0)))

    class _Rec:
        def __init__(self):
            self.ops = []

        def __getattr__(self, name):
            def f(*a_, **k_):
                self.ops.append((name, a_, k_))
            return f

    R_ = _Rec()
    results = {}

    def dve_part(g, q0, Lq, s0, Ls, out):
        w = WINS[g]
        Wd = Ls + 16
        base = s0 - 8
        lo = max(q0, base)
        hi = min(q0 + Lq, s0 + Ls + 8)
        for cc in range(2):
            c = g * 2 + cc
            v = vb.next()
            R_.dma("sp", v.t[:, lo - base:hi - base], pvT[c, :, lo:hi], v, [], [v])
            vp_ = vp.next()
            if lo > base:
                R_.memset(vp_.t[:, 0:lo - base], 0.0, [], [vp_])
            if hi < base + Wd:
                R_.memset(vp_.t[:, hi - base:Wd], 0.0, [], [vp_])
            R_.copy(vp_.t[:, lo - base:hi - base], v.t[:, lo - base:hi - base], [v, vp_], [vp_])
            cur = vp_
            ln = Wd
            for s_ in range(g + 1):
                sh = 1 << s_
                nx = sr.next()
                ln = ln - sh
                R_.tt(nx.t[:, 0:ln], cur.t[:, 0:ln], cur.t[:, sh:sh + ln], ALU.add, [cur], [nx])
                cur = nx
            off = 8 - w // 2
            pl_ = plr.next()
            R_.stt(pl_.t[:, 0:Ls], cur.t[:, off:off + Ls], 1.0 / w, vp_.t[:, 8:8 + Ls], ALU.mult, ALU.subtract,
                  [cur, vp_], [pl_])
            if s0 == q0 or s0 + Ls == q0 + Lq:
                et = etr.next()
                if s0 == q0:
                    R_.tt(et.t[:, 0:8], cur.t[:, off:off + 8], edge.t[:, g * 16:g * 16 + 8], ALU.mult,
                         [cur, edge, et], [et])
                    R_.tt(pl_.t[:, 0:8], et.t[:, 0:8], vp_.t[:, 8:16], ALU.subtract, [et, vp_, pl_], [pl_])
                if s0 + Ls == q0 + Lq:
                    R_.tt(et.t[:, 8:16], cur.t[:, off + Ls - 8:off + Ls], edge.t[:, g * 16 + 8:g * 16 + 16],
                         ALU.mult, [cur, edge, et], [et])
                    R_.tt(pl_.t[:, Ls - 8:Ls], et.t[:, 8:16], vp_.t[:, Ls:Ls + 8], ALU.subtract,
                         [et, vp_, pl_], [pl_])
            out.append(pl_)

    def pe_part(g, q0, Lq, s0, Ls, pls):
        for dt in range(2):
            c2 = g * 2 + dt
            pg_ = pgr.next()
            R_.dma("sp", pg_.t[:, 0:Ls], pgT[c2, :, s0:s0 + Ls], pg_, [], [pg_])
            for blk in range(0, Ls, 512):
                n = min(512, Ls - blk)
                R_.mm(pB.t[:, 0:n], [(pwb.t[:, g, cc, dt * 128:(dt + 1) * 128], pls[cc].t[:, blk:blk + n])
                                    for cc in range(2)], [pwb] + pls, [pB])
                st = pst.next()
                R_.stt(st.t[:, 0:n], pB.t[:, 0:n], psc.t[:, l * 8 + c2:l * 8 + c2 + 1], pg_.t[:, blk:blk + n],
                      ALU.mult, ALU.mult, [pB, psc, pg_], [st])
                R_.dma("sp", mT[c2, :, s0 + blk:s0 + blk + n], st.t[:, 0:n], st, [st], [])

    for i, u in enumerate(units):
        if i > 0:
            pe_part(*units[i - 1], results[i - 1])
        outl = []
        dve_part(*u, outl)
        results[i] = outl
    pe_part(*units[-1], results[len(units) - 1])
    micro = []
    ops = R_.ops
    j = 0
    while j < len(ops):
        grp = [ops[j]]
        if ops[j][0] == "mm" and j + 1 < len(ops):
            grp.append(ops[j + 1])
        j += len(grp)

        def run(grp=grp):
            for (name, a_, k_) in grp:
                getattr(P, name)(*a_, **k_)
        micro.append(run)
    return micro


def phase_attn(nc, P, l, with_ctx, G):
    qT, qcT, kT, Vs, agT, mT = G["qT"], G["qcT"], G["kT"], G["Vs"], G["agT"], G["mT"]
    onesb, neglam, sws, selb = G["onesb"], G["neglam"], G["sws"], G["selb"]
    with ExitStack() as ph:
        kTh = Ring(P, ph, "kTh", 2, [128, LT], BF16, dma=True)
        Vh = Ring(P, ph, "Vh", 2, [128, NT * 128], BF16, dma=True)
        qTh = Ring(P, ph, "qTh", 2, [128, L], BF16, dma=True)
        agh = Ring(P, ph, "agh", 2, [128, LT], BF16, dma=True)
        qch = Ring(P, ph, "qch", 2, [128, C], BF16, dma=True)
        Eb = Ring(P, ph, "Eb", 3, [128, 1024], BF16)
        orA = Ring(P, ph, "orA", 2, [128, 512], F32)
        orB = Ring(P, ph, "orB", 2, [128, 512], F32)
        zraw = Ring(P, ph, "zraw", 2, [64, 512], F32)
        rzf = Ring(P, ph, "rzf", 2, [64, 512], F32)
        rzh = Ring(P, ph, "rzh", 2, [64, 512], BF16)
        rzd = Ring(P, ph, "rzd", 2, [64, 512], F32)
        rzl = Ring(P, ph, "rzl", 2, [64, 512], BF16)
        t0r = Ring(P, ph, "t0r", 2, [128, 512], F32)
        t1r = Ring(P, ph, "t1r", 2, [128, 512], F32)
        ob = Ring(P, ph, "ob", 2, [128, 512], F32)
        sqb = Ring(P, ph, "sqb", 2, [128, 512], BF16)
        rsb = Ring(P, ph, "rsb", 2, [128, 512], F32)
        ub = Ring(P, ph, "ub", 2, [128, 512], F32)
        ost = Ring(P, ph, "ost", 2, [128, 512], BF16, dma=True)
        pS = Ring(P, ph, "pS", 2, [128, 1024], F32, psum=True)
        pO = Ring(P, ph, "pO", 1, [128, 1024], F32, psum=True).next()
        pZ = Ring(P, ph, "pZ", 1, [128, 512], F32, psum=True).next()
        pB = Ring(P, ph, "pB", 1, [128, 512], F32, psum=True).next()

        def v2(ap, N):
            return ap.rearrange("p (m n) -> p m n", m=2)[:, :, 0:N]

        pool_tasks = make_pool_tasks(nc, P, l, with_ctx, G, ph, pB)
        blk_count = [0]

        stages = []

        def run_stages(upto):
            while stages and stages[0][0] <= upto:
                stages.pop(0)[1]()

        for h in range(H):
            k_ = kTh.next()
            P.dma("sp", k_.t[:], kT[h], k_, [], [k_])
            q_ = qTh.next()
            P.dma("sp", q_.t[:], qT[h], q_, [], [q_])
            v_ = Vh.next()
            P.dma("sp", v_.t[:], Vs[h].rearrange("p t e -> p (t e)"), v_, [], [v_])
            a_ = agh.next()
            P.dma("sp", a_.t[:], agT[h], a_, [], [a_])
            if with_ctx:
                qc_ = qch.next()
                P.dma("sp", qc_.t[:], qcT[h], qc_, [], [qc_])
            blocks = []
            if with_ctx:
                blocks.append((qc_, qc_.t[:, 0:C], 0, C, [0, 1]))
            for qb in range(8):
                blocks.append((q_, q_.t[:, qb * 512:(qb + 1) * 512], C + qb * 512, 512, list(range(NT))))
            for (qbuf, qsrc, tok0, N, kts) in blocks:
                nk = len(kts)
                if nk < 10:
                    run_stages(10 ** 9)

                def S_op(kt):
                    ps = pS.next()
                    P.mm_multi([(ps.t[:, 0:N], k_.t[0:64, kt * 128:(kt + 1) * 128], qsrc[0:64, :], True, True, None),
                                (ps.t[:, 512:512 + N], k_.t[64:128, kt * 128:(kt + 1) * 128], qsrc[64:128, :],
                                 True, True, None)], [k_, qbuf], [ps])
                    return ps

                ps_list = {0: S_op(kts[0])}
                if nk > 1:
                    ps_list[1] = S_op(kts[1])
                for i, kt in enumerate(kts):
                    ps = ps_list.pop(i)
                    e_ = Eb.next()
                    P.act(v2(e_.t[:], N), v2(ps.t[:], N), AF.Exp, [ps], [e_], scale=0.125)
                    run_stages(i)
                    if i + 2 < nk:
                        ps_list[i + 2] = S_op(kts[i + 2])
                    first = (i == 0)
                    last = (i == nk - 1)
                    vt = v_.t[:, kt * 128:(kt + 1) * 128]
                    items = [(pO.t[:, 0:N], vt, e_.t[:, 0:N], first, last, None),
                             (pO.t[:, 512:512 + N], vt, e_.t[:, 512:512 + N], first, last, None),
                             (pZ.t[0:32, 0:N], onesb.t[:, 0:32], e_.t[:, 0:N], first, last, (0, 0)),
                             (pZ.t[32:64, 0:N], onesb.t[:, 0:32], e_.t[:, 512:512 + N], first, last, (0, 32))]
                    P.mm_multi(items, [e_, v_, onesb], [pO, pZ] if (first or last) else [])
                    if i % 3 == 2 and pool_tasks:
                        pool_tasks.pop(0)()
                run_stages(10 ** 9)
                oa = orA.next()
                P.act(oa.t[:, 0:N], pO.t[:, 0:N], AF.Copy, [pO], [oa])
                obb = orB.next()
                P.copy(obb.t[:, 0:N], pO.t[:, 512:512 + N], [pO], [obb])
                zr = zraw.next()
                P.act(zr.t[0:64, 0:N], pZ.t[0:64, 0:N], AF.Copy, [pZ], [zr])

                def stageA(oa=oa, zr=zr, N=N):
                    rf = rzf.next()
                    P.recip(rf.t[0:64, 0:N], zr.t[0:64, 0:N], [zr], [rf])
                    rh = rzh.next()
                    P.copy(rh.t[0:64, 0:N], rf.t[0:64, 0:N], [rf], [rh])
                    rd = rzd.next()
                    P.tt(rd.t[0:64, 0:N], rf.t[0:64, 0:N], rh.t[0:64, 0:N], ALU.subtract, [rf, rh], [rd])
                    rl = rzl.next()
                    P.copy(rl.t[0:64, 0:N], rd.t[0:64, 0:N], [rd], [rl])
                    P.mm(pB.t[:, 0:N], [(selb.t[0:64, 0:128], rh.t[0:64, 0:N]), (selb.t[0:64, 0:128], rl.t[0:64, 0:N])],
                         [selb, rh, rl], [pB])
                    t0 = t0r.next()
                    P.tt(t0.t[:, 0:N], oa.t[:, 0:N], pB.t[:, 0:N], ALU.mult, [oa, pB], [t0])
                    return rh, rl, t0

                def stageB(obb=obb, N=N, carry={}):
                    rh, rl, t0 = carry["a"]
                    P.mm(pB.t[:, 0:N], [(selb.t[0:64, 128:256], rh.t[0:64, 0:N]),
                                        (selb.t[0:64, 128:256], rl.t[0:64, 0:N])], [selb, rh, rl], [pB])
                    t1 = t1r.next()
                    P.tt(t1.t[:, 0:N], obb.t[:, 0:N], pB.t[:, 0:N], ALU.mult, [obb, pB], [t1])
                    o_ = ob.next()
                    P.stt(o_.t[:, 0:N], t1.t[:, 0:N], neglam.t[:, l:l + 1], t0.t[:, 0:N], ALU.mult, ALU.add,
                          [t1, t0, neglam], [o_])
                    sq_ = sqb.next()
                    P.tt(sq_.t[:, 0:N], o_.t[:, 0:N], o_.t[:, 0:N], ALU.mult, [o_], [sq_])
                    return o_, sq_

                def stageC(N=N, tok0=tok0, a_=a_, h=h, carry={}):
                    o_, sq_ = carry["b"]
                    P.mm(pB.t[:, 0:N], [(onesb.t[:], sq_.t[:, 0:N])], [sq_, onesb], [pB])
                    rs_ = rsb.next()
                    P.act(rs_.t[:, 0:N], pB.t[:, 0:N], AF.Ln, [pB], [rs_], scale=1.0 / 128, bias=EPS)
                    P.act(rs_.t[:, 0:N], rs_.t[:, 0:N], AF.Exp, [rs_], [rs_], scale=-0.5)
                    u_ = ub.next()
                    P.stt(u_.t[:, 0:N], o_.t[:, 0:N], sws.t[:, l:l + 1], rs_.t[:, 0:N], ALU.mult, ALU.mult,
                          [o_, sws, rs_], [u_])
                    st = ost.next()
                    P.tt(st.t[:, 0:N], u_.t[:, 0:N], a_.t[:, tok0:tok0 + N], ALU.mult, [u_, a_], [st])
                    P.dma("sp", mT[8 + h, :, tok0:tok0 + N], st.t[:, 0:N], st, [st], [])

                carry = {}

                def mkA(f=stageA, carry=carry):
                    carry["a"] = f()

                def mkB(f=stageB, carry=carry):
                    carry["b"] = f(carry=carry)

                def mkC(f=stageC, carry=carry):
                    f(carry=carry)

                stages.extend([(3, mkA), (5, mkB), (8, mkC)])
        run_stages(10 ** 9)
        while pool_tasks:
            pool_tasks.pop(0)()


def phase_out(nc, P, l, x_src, c_src, with_ctx, G):
    modd, w_out, mT, y, ctx1 = G["modd"], G["w_out"], G["mT"], G["y"], G["ctx1"]
    with ExitStack() as ph:
        S = ph.enter_context
        wo = Buf("wo", S(SBT(nc, "wo", [128, KC, D], BF16)), P.new_dsem())
        gx = Buf("gx", S(SBT(nc, "gx", [128, D], F32)), P.new_dsem())
        gc = Buf("gc", S(SBT(nc, "gc", [128, D], F32)), P.new_dsem())
        mTb = Ring(P, ph, "mTb", 2, [128, KC, 512], BF16, dma=True)
        xring = Ring(P, ph, "xo", 2, [128, D], F32, dma=True)
        tr = Ring(P, ph, "tr", 2, [128, 512], F32)
        ot = Ring(P, ph, "ot", 2, [128, D], F32, dma=True)
        po = Ring(P, ph, "po", 4, [128, 512], F32, psum=True)
        wosrc = G["w_out_bf"][l].rearrange("(k p) n -> p k n", p=128)
        P.dma_multi("pool", [(wo.t[:, 2 * j:2 * j + 2, :], wosrc[:, 2 * j:2 * j + 2, :]) for j in range(8)],
                    wo, [G["wbf_out"][l]], [wo])
        P.dma("sp", gx.t[:], modd[l, 0, 2 * D:3 * D].partition_broadcast(128), gx, [], [gx])
        P.dma("sp", gc.t[:], modd[l, 1, 2 * D:3 * D].partition_broadcast(128), gc, [], [gc])
        blks = blocks_of_tiles(True)
        if not with_ctx:
            blks = blks[1:]
        for tiles in blks:
            N = 128 * len(tiles)
            tok0 = tiles[0] * 128
            mb = mTb.next()
            msrc = mT[:, :, tok0:tok0 + N].rearrange("c p t -> p c t")
            P.dma_multi("sp", [(mb.t[:, 4 * j:4 * j + 4, 0:N], msrc[:, 4 * j:4 * j + 4, :]) for j in range(4)],
                        mb, [], [mb])
            for pos, T in enumerate(tiles):
                xt = xring.next()
                if T < 2:
                    src = c_src[T * 128:(T + 1) * 128, :]
                    dst = ctx1[T * 128:(T + 1) * 128, :]
                    g_ = gc
                else:
                    src = x_src[(T - 2) * 128:(T - 1) * 128, :]
                    dst = y[(T - 2) * 128:(T - 1) * 128, :]
                    g_ = gx
                P.dma("sp", xt.t[:], src, xt, [], [xt])
                ot_ = ot.next()
                for n in range(4):
                    ps = po.next()
                    P.mm(ps.t[:, :], [(mb.t[:, k, pos * 128:(pos + 1) * 128], wo.t[:, k, n * 512:(n + 1) * 512])
                                      for k in range(KC)], [mb, wo], [ps])
                    t_ = tr.next()
                    P.tt(t_.t[:], ps.t[:], g_.t[:, n * 512:(n + 1) * 512], ALU.mult, [ps, g_], [t_])
                    P.tt(ot_.t[:, n * 512:(n + 1) * 512], t_.t[:], xt.t[:, n * 512:(n + 1) * 512], ALU.add,
                         [t_, xt, ot_], [ot_])
                P.dma("sp", dst, ot_.t[:], ot_, [ot_], [])


def _consts():
    ident = np.eye(128, dtype=np.float32)
    inv = (np.float32(10000.0) ** (-np.arange(16, dtype=np.float32) / np.float32(16))).astype(np.float32)
    cos = np.ones((128, NT, 32), np.float32)
    sin = np.zeros((128, NT, 32), np.float32)
    t = (np.arange(NT - 2)[None, :] * 128 + np.arange(128)[:, None])
    row = (t // 64).astype(np.float32)
    col = (t % 64).astype(np.float32)
    ar = row[:, :, None] * inv
    ac = col[:, :, None] * inv
    cos[:, 2:, 0:16] = np.cos(ar)
    cos[:, 2:, 16:32] = np.cos(ac)
    sin[:, 2:, 0:16] = np.sin(ar)
    sin[:, 2:, 16:32] = np.sin(ac)
    edge = np.zeros((4, 16), np.float32)
    for g, w in enumerate(WINS):
        for i in range(8):
            edge[g, i] = 1.0 / min(w, i + w // 2)
            edge[g, 8 + i] = 1.0 / min(w, (8 - i) + w // 2)
    return ident, cos.reshape(128, NT * 32), sin.reshape(128, NT * 32), edge.reshape(64)


def make_in_maps(x, c, ctx, c_ctx, norm_w, w_ada, b_ada, w_in, pool_w, pool_scale, q_norm_w, k_norm_w,
                 lambda_q1, lambda_k1, lambda_q2, lambda_k2, subln_w, w_out):
    f = lambda a: np.ascontiguousarray(np.asarray(a, dtype=np.float32))
    ident, cos, sin, edge = _consts()
    pscT = f(np.asarray(pool_scale).reshape(NLAYERS, 8, 128).transpose(2, 0, 1).reshape(128, 16))
    swT = f(np.asarray(subln_w).T)
    lams = f(np.stack([np.asarray(lambda_q1), np.asarray(lambda_k1), np.asarray(lambda_q2),
                       np.asarray(lambda_k2)], axis=1).reshape(-1))
    shared = {
        "norm_w": f(norm_w), "w_ada": f(w_ada), "b_ada": f(b_ada), "w_in": f(w_in), "pool_w": f(pool_w),
        "pscT": pscT, "qw": f(np.asarray(q_norm_w).reshape(-1)), "kw": f(np.asarray(k_norm_w).reshape(-1)),
        "lams": lams, "swT": swT, "w_out": f(w_out), "ident": ident, "rope_cos": cos, "rope_sin": sin,
        "edge_rc": edge,
    }
    x = np.asarray(x)
    c = np.asarray(c)
    ctx = np.asarray(ctx)
    c_ctx = np.asarray(c_ctx)
    maps = []
    for b in range(x.shape[0]):
        cond = np.stack([c[b], c_ctx], axis=-1).reshape(KC, 128, 2).transpose(1, 0, 2).reshape(128, 32)
        m = dict(shared)
        m["x"] = f(x[b])
        m["ctx"] = f(ctx[b])
        m["cond"] = f(cond)
        maps.append(m)
    return maps


def kernel(**inputs):
    maps = make_in_maps(**inputs)
    nc = build()
    res = run_bass_kernel_spmd(nc, maps, core_ids=list(range(len(maps))))
    return np.stack([np.asarray(r["y"], dtype=np.float32) for r in res.results], axis=0)
```

```python
import math
from contextlib import ExitStack

import numpy as np

import concourse.bass as bass
import concourse.mybir as mybir
from concourse.bass_utils import run_bass_kernel_spmd

F32 = mybir.dt.float32
BF16 = mybir.dt.bfloat16
ALU = mybir.AluOpType
AF = mybir.ActivationFunctionType
AX = mybir.AxisListType

D = 2048
KC = 16
L = 4096
C = 256
LT = L + C
NT = LT // 128
H = 8
DIN = 6144
EPS = 1e-6
WINS = (2, 4, 8, 16)
NLAYERS = 2

DEBUG_SCRATCH = False
STOP_AFTER = None
N_DSEMS = 60


def lam_init_of(l):
    return 0.8 - 0.6 * math.exp(-0.3 * l)


_UNIQ = [0]


def SBT(nc, name, shape, dt):
    _UNIQ[0] += 1
    return nc.sbuf_tensor("%s_s%d" % (name, _UNIQ[0]), shape, dt)


def PST(nc, name, shape, dt):
    _UNIQ[0] += 1
    return nc.psum_tensor("%s_p%d" % (name, _UNIQ[0]), shape, dt)


class Sem:
    def __init__(self, h):
        self.h = h
        self.val = 0


class Buf:
    def __init__(self, name, t=None, dsem=None):
        self.name = name
        self.t = t
        self.dsem = dsem
        self.w = {}
        self.r = {}


class Queue:
    def __init__(self, name):
        self.name = name
        self.sem = None
        self.ops = []
        self.seen = {}


def _merge(dst, src):
    for s, v in src.items():
        if dst.get(s, 0) < v:
            dst[s] = v


class Prog:
    def __init__(self, nc, es):
        self.nc = nc
        self.qs = {n: Queue(n) for n in ("pe", "act", "dve", "pool", "sp")}
        for n in ("pe", "act", "dve"):
            self.qs[n].sem = Sem(es.enter_context(nc.semaphore("c_" + n)))
        self.dsems = [Sem(es.enter_context(nc.semaphore("d%d" % i))) for i in range(N_DSEMS)]
        self.dnext = 0
        self.dnext_sw = 0
        self.nops = 0

    def new_dsem(self, kind="sp"):
        if kind == "pool":
            s = self.dsems[N_DSEMS - 1 - self.dnext_sw]
            self.dnext_sw += 1
            return s
        s = self.dsems[self.dnext]
        self.dnext += 1
        assert self.dnext + 12 <= N_DSEMS
        return s

    def phase_reset(self):
        self.dnext = 0

    def op(self, qn, fn, R=(), W=(), dsem=None):
        q = self.qs[qn]
        need = {}
        for b in R:
            _merge(need, b.w)
        for b in W:
            _merge(need, b.w)
            _merge(need, b.r)
        waits = []
        for s, v in need.items():
            if q.seen.get(s, 0) < v:
                waits.append((s, v))
                q.seen[s] = v
        if dsem is not None:
            sem = dsem
            sem.val += 16
            inc = 16
        else:
            sem = q.sem
            sem.val += 1
            inc = 1
        val = sem.val
        for b in R:
            if b.r.get(sem, 0) < val:
                b.r[sem] = val
        for b in W:
            b.w = {sem: val}
            b.r = {}
        q.ops.append((waits, fn, sem, inc))
        self.nops += 1

    def barrier(self):
        sems = [self.qs[n].sem for n in ("pe", "act", "dve")] + self.dsems
        for q in self.qs.values():
            waits = []
            for s in sems:
                if s.val > q.seen.get(s, 0):
                    waits.append((s, s.val))
                    q.seen[s] = s.val
            q.ops.append((waits, None, None, 0))

    def emit(self):
        nc = self.nc
        with nc.Block() as block:
            for name, deco in (("sp", block.sync), ("act", block.scalar), ("dve", block.vector),
                               ("pool", block.gpsimd), ("pe", block.tensor)):
                q = self.qs[name]
                ops = q.ops
                q.ops = []

                def body(eng, ops=ops):
                    for waits, fn, sem, inc in ops:
                        for s, v in waits:
                            eng.wait_ge(s.h, v)
                        if fn is not None:
                            ins = fn(eng)
                            ins.then_inc(sem.h, inc)

                deco(body)

    def tt(self, out, in0, in1, op, R, W, q="dve"):
        self.op(q, lambda e: e.tensor_tensor(out=out, in0=in0, in1=in1, op=op), R, W)

    def ts(self, out, in0, s1, s2, op0, op1, R, W, q="dve"):
        self.op(q, lambda e: e.tensor_scalar(out=out, in0=in0, scalar1=s1, scalar2=s2, op0=op0, op1=op1), R, W)

    def stt(self, out, in0, scalar, in1, op0, op1, R, W, q="dve"):
        self.op(q, lambda e: e.scalar_tensor_tensor(out=out, in0=in0, scalar=scalar, in1=in1, op0=op0, op1=op1), R, W)

    def copy(self, out, in_, R, W, q="dve"):
        self.op(q, lambda e: e.tensor_copy(out=out, in_=in_), R, W)

    def red(self, out, in_, R, W, q="dve"):
        self.op(q, lambda e: e.tensor_reduce(out=out, in_=in_, axis=AX.X, op=ALU.add), R, W)

    def recip(self, out, in_, R, W):
        self.op("dve", lambda e: e.reciprocal(out=out, in_=in_), R, W)

    def memset(self, ap, val, R, W, q="dve"):
        self.op(q, lambda e: e.memset(ap, val), R, W)

    def act(self, out, in_, func, R, W, scale=1.0, bias=0.0, accum=None):
        if accum is None:
            self.op("act", lambda e: e.activation(out=out, in_=in_, func=func, scale=scale, bias=bias), R, W)
        else:
            self.op("act", lambda e: e.activation(out=out, in_=in_, func=func, scale=scale, bias=bias,
                                                  accum_out=accum), R, W)

    def dma(self, q, out, in_, buf, R, W):
        self.op(q, lambda e: e.dma_start(out=out, in_=in_), R, W, dsem=buf.dsem)

    def dma_multi(self, q, pairs, buf, R, W):
        sem = buf.dsem
        n = len(pairs)

        def fn(e):
            ins = None
            for i, (o, a) in enumerate(pairs):
                ins = e.dma_start(out=o, in_=a)
                if i < n - 1:
                    ins.then_inc(sem.h, 16)
            return ins
        sem.val += 16 * (n - 1)
        self.op(q, fn, R, W, dsem=sem)

    def mm(self, out, pairs, R, W, start=True, stop=True):
        def fn(e):
            n = len(pairs)
            ins = None
            for i, (a, b) in enumerate(pairs):
                ins = e.matmul(out, lhsT=a, rhs=b, start=(start and i == 0), stop=(stop and i == n - 1))
            return ins
        self.op("pe", fn, R, W)

    def mm_multi(self, items, R, W):
        def fn(e):
            ins = None
            for (o, a, b, st, sp, tp) in items:
                if tp is None:
                    ins = e.matmul(o, lhsT=a, rhs=b, start=st, stop=sp)
                else:
                    ins = e.matmul(o, lhsT=a, rhs=b, start=st, stop=sp, tile_position=tp)
            return ins
        self.op("pe", fn, R, W)

    def transposes(self, items, ident, R, W):
        def fn(e):
            ins = None
            for (o, i) in items:
                ins = e.transpose(out=o, in_=i, identity=ident)
            return ins
        self.op("pe", fn, R, W)


class Ring:
    def __init__(self, prog, es, name, n, shape, dtype, dma=False, psum=False):
        nc = prog.nc
        self.slots = []
        for i in range(n):
            if psum:
                t = es.enter_context(PST(nc, "%s%d" % (name, i), shape, dtype))
            else:
                t = es.enter_context(SBT(nc, "%s%d" % (name, i), shape, dtype))
            self.slots.append(Buf("%s%d" % (name, i), t,
                                  prog.new_dsem("pool" if dma == "pool" else "sp") if dma else None))
        self.i = 0
        self.n = n

    def next(self):
        b = self.slots[self.i % self.n]
        self.i += 1
        return b


def bc(ap, shape):
    return ap.to_broadcast(shape)


def build():
    nc = bass.Bass("TRN2", target_bir_lowering=False)
    skind = "ExternalOutput" if DEBUG_SCRATCH else "Internal"

    def din(name, shape, dt=F32):
        return nc.dram_tensor(name, shape, dt, kind="ExternalInput").ap()

    def dscr(name, shape, dt):
        return nc.dram_tensor(name, shape, dt, kind=skind).ap()

    x_in = din("x", [L, D])
    ctx_in = din("ctx", [C, D])
    cond_in = din("cond", [128, 32])
    norm_w = din("norm_w", [NLAYERS, D])
    w_ada = din("w_ada", [NLAYERS, D, DIN])
    b_ada = din("b_ada", [NLAYERS, DIN])
    w_in = din("w_in", [NLAYERS, D, DIN])
    pool_w = din("pool_w", [NLAYERS, 4, 256, 256])
    psc_in = din("pscT", [128, 16])
    qw_in = din("qw", [NLAYERS * 64])
    kw_in = din("kw", [NLAYERS * 64])
    lams_in = din("lams", [NLAYERS * 4 * 64])
    sw_in = din("swT", [128, NLAYERS])
    w_out = din("w_out", [NLAYERS, D, D])
    ident_in = din("ident", [128, 128])
    cos_in = din("rope_cos", [128, NT * 32])
    sin_in = din("rope_sin", [128, NT * 32])
    edge_in = din("edge_rc", [64])

    y = nc.dram_tensor("y", [L, D], F32, kind="ExternalOutput").ap()

    modd = dscr("modd", [NLAYERS, 2, DIN], F32)
    qT = dscr("qT", [H, 128, L], BF16)
    qcT = dscr("qcT", [H, 128, C], BF16)
    kT = dscr("kT", [H, 128, LT], BF16)
    Vs = dscr("Vs", [H, 128, NT, 128], BF16)
    pvT = dscr("pvT", [8, 128, LT], BF16)
    pgT = dscr("pgT", [8, 128, LT], BF16)
    agT = dscr("agT", [8, 128, LT], BF16)
    mT = dscr("mT", [16, 128, LT], BF16)
    ctx1 = dscr("ctx1", [C, D], F32)
    w_in_bf = nc.dram_tensor("w_in_bf", [NLAYERS, D, DIN], BF16, kind="Internal").ap()
    w_out_bf = nc.dram_tensor("w_out_bf", [NLAYERS, D, D], BF16, kind="Internal").ap()

    with ExitStack() as top:
        P = Prog(nc, top)
        E = top.enter_context

        def pers(name, shape, dt, dma=False):
            t = E(SBT(nc, name, shape, dt))
            return Buf(name, t, P.new_dsem() if dma else None)

        identf = pers("identf", [128, 128], F32, dma=True)
        identb = pers("identb", [128, 128], BF16)
        onesb = pers("onesb", [128, 128], BF16)
        cosT = pers("cosT", [128, NT * 32], F32, dma=True)
        sinT = pers("sinT", [128, NT * 32], F32, dma=True)
        edge = pers("edge", [128, 64], F32, dma=True)
        psc = pers("psc", [128, 16], F32, dma=True)
        sws = pers("sws", [128, NLAYERS], F32, dma=True)
        qw = pers("qw", [128, NLAYERS * 64], F32, dma=True)
        kw = pers("kw", [128, NLAYERS * 64], F32, dma=True)
        lamt = pers("lamt", [128, NLAYERS * 4 * 64], F32, dma=True)
        lpr = pers("lpr", [128, 4 * 64], F32)
        ls4 = pers("ls4", [128, 4], F32)
        le4 = pers("le4", [128, 4], F32)
        nl0 = pers("nl0", [128, NLAYERS], F32)
        neglam = pers("neglam", [128, NLAYERS], F32)
        wbf_in = [Buf("wbf_in%d" % l_, None, P.new_dsem("pool")) for l_ in range(NLAYERS)]
        wbf_out = [Buf("wbf_out%d" % l_, None, P.new_dsem("pool")) for l_ in range(NLAYERS)]
        n_pers_dsems = P.dnext
        n_pers_dsems_sw = P.dnext_sw

        def conv_in(l_):
            P.dma_multi("pool", [(w_in_bf[l_][r * 128:(r + 1) * 128, :], w_in[l_][r * 128:(r + 1) * 128, :])
                                 for r in range(KC)], wbf_in[l_], [], [wbf_in[l_]])

        def conv_out(l_):
            P.dma_multi("pool", [(w_out_bf[l_][r * 128:(r + 1) * 128, :], w_out[l_][r * 128:(r + 1) * 128, :])
                                 for r in range(KC)], wbf_out[l_], [], [wbf_out[l_]])


        P.dma("sp", identf.t[:], ident_in, identf, [], [identf])
        P.dma("sp", cosT.t[:], cos_in, cosT, [], [cosT])
        P.dma("sp", sinT.t[:], sin_in, sinT, [], [sinT])
        P.dma("sp", edge.t[:], edge_in.partition_broadcast(128), edge, [], [edge])
        P.dma("sp", psc.t[:], psc_in, psc, [], [psc])
        P.dma("sp", sws.t[:], sw_in, sws, [], [sws])
        P.dma("sp", qw.t[:], qw_in.partition_broadcast(128), qw, [], [qw])
        P.dma("sp", kw.t[:], kw_in.partition_broadcast(128), kw, [], [kw])
        P.dma("sp", lamt.t[:], lams_in.partition_broadcast(128), lamt, [], [lamt])
        selb = pers("selb", [64, 256], BF16)
        P.memset(selb.t[:], 0.0, [], [selb])
        P.memset(selb.t[0:1, 0:128], 1.0, [selb], [selb])
        P.memset(selb.t[32:33, 128:256], 1.0, [selb], [selb])
        P.copy(identb.t[:], identf.t[:], [identf], [identb])
        P.memset(onesb.t[:], 1.0, [], [onesb])
        for l in range(NLAYERS):
            P.ts(sws.t[:, l:l + 1], sws.t[:, l:l + 1], float(1.0 - lam_init_of(l)), 0.0, ALU.mult, ALU.add,
                 [sws], [sws])
        lv = lamt.t[:].rearrange("p (a j d) -> p a j d", j=2, d=64)
        P.tt(lpr.t[:].rearrange("p (a d) -> p a d", d=64), lv[:, :, 0, :], lv[:, :, 1, :], ALU.mult, [lamt], [lpr])
        P.red(ls4.t[:], lpr.t[:].rearrange("p (a d) -> p a d", d=64), [lpr], [ls4])
        P.act(le4.t[:], ls4.t[:], AF.Exp, [ls4], [le4])
        for l in range(NLAYERS):
            P.tt(nl0.t[:, l:l + 1], le4.t[:, 2 * l + 1:2 * l + 2], le4.t[:, 2 * l:2 * l + 1], ALU.subtract,
                 [le4], [nl0])
            P.ts(neglam.t[:, l:l + 1], nl0.t[:, l:l + 1], 1.0, float(-lam_init_of(l)), ALU.mult, ALU.add,
                 [nl0], [neglam])

        def end_phase():
            P.barrier()
            P.emit()
            P.dnext = n_pers_dsems
            P.dnext_sw = n_pers_dsems_sw

        with ExitStack() as ph:
            cs = Buf("cs", ph.enter_context(SBT(nc, "cs", [128, 32], F32)), P.new_dsem())
            e1 = Buf("e1", ph.enter_context(SBT(nc, "e1", [128, 32], F32)))
            scb = Buf("scb", ph.enter_context(SBT(nc, "scb", [128, 32], BF16)))
            bsb = Buf("bsb", ph.enter_context(SBT(nc, "bsb", [2, DIN], F32)), P.new_dsem())
            modsb = Buf("modsb", ph.enter_context(SBT(nc, "modsb", [2, DIN], F32)), P.new_dsem())
            wa = Ring(P, ph, "wa", 3, [128, KC, 512], BF16, dma="pool")
            pmod = Ring(P, ph, "pmod", 2, [128, 512], F32, psum=True)
            P.dma("sp", cs.t[:], cond_in, cs, [], [cs])
            P.act(e1.t[:], cs.t[:], AF.Exp, [cs], [e1], scale=-1.0)
            P.act(e1.t[:], e1.t[:], AF.Ln, [e1], [e1], scale=1.0, bias=1.0)
            P.act(e1.t[:], e1.t[:], AF.Exp, [e1], [e1], scale=-1.0)
            P.tt(scb.t[:], cs.t[:], e1.t[:], ALU.mult, [cs, e1], [scb])
            for l in range(NLAYERS):
                P.dma("sp", bsb.t[:], b_ada[l].partition_broadcast(2), bsb, [], [bsb])
                for n in range(DIN // 512):
                    w = wa.next()
                    wsrc = w_ada[l][:, n * 512:(n + 1) * 512].rearrange("(k p) n -> p k n", p=128)
                    P.dma_multi("pool", [(w.t[:, 4 * j:4 * j + 4, :], wsrc[:, 4 * j:4 * j + 4, :]) for j in range(4)],
                                w, [], [w])
                    pm = pmod.next()
                    P.mm(pm.t[0:2, :], [(scb.t[:, 2 * k:2 * k + 2], w.t[:, k, :]) for k in range(KC)],
                         [scb, w], [pm])
                    P.tt(modsb.t[0:2, n * 512:(n + 1) * 512], pm.t[0:2, :], bsb.t[0:2, n * 512:(n + 1) * 512],
                         ALU.add, [pm, bsb], [modsb])
                P.dma("sp", modd[l], modsb.t[:], modsb, [modsb], [])
                if l == 0:
                    conv_in(0)
            end_phase()
        if STOP_AFTER == (0, "P0"):
            return nc

        for l in range(NLAYERS):
            x_src = x_in if l == 0 else y
            c_src = ctx_in if l == 0 else ctx1
            with_ctx = (l == 0)
            phase_A(nc, P, l, x_src, c_src, with_ctx, locals())
            end_phase()
            if STOP_AFTER == (l, "A"):
                return nc
            if l == 0:
                conv_out(0)
                for l2 in range(1, NLAYERS):
                    conv_in(l2)
                    conv_out(l2)
            phase_attn(nc, P, l, with_ctx, locals())
            end_phase()
            if STOP_AFTER == (l, "T"):
                return nc
            phase_out(nc, P, l, x_src, c_src, with_ctx, locals())
            end_phase()
            if STOP_AFTER == (l, "O"):
                return nc
    return nc


def blocks_of_tiles(with_ctx_full):
    blks = [[0, 1]]
    for j in range(8):
        blks.append([2 + 4 * j + i for i in range(4)])
    return blks


def phase_A(nc, P, l, x_src, c_src, with_ctx, G):
    modd, norm_w, w_in = G["modd"], G["norm_w"], G["w_in"]
    identb, cosT, sinT, qw, kw = G["identb"], G["cosT"], G["sinT"], G["qw"], G["kw"]
    pvT, pgT, agT, qT, qcT, kT, Vs = G["pvT"], G["pgT"], G["agT"], G["qT"], G["qcT"], G["kT"], G["Vs"]
    with ExitStack() as ph:
        S = ph.enter_context

        def sb(name, shape, dt, dma=False):
            return Buf(name, S(SBT(nc, name, shape, dt)), P.new_dsem() if dma else None)

        Gx = sb("Gx", [128, D], F32, True)
        Sx = sb("Sx", [128, D], F32, True)
        Gc = sb("Gc", [128, D], F32, True)
        Sc = sb("Sc", [128, D], F32, True)
        junk = S(SBT(nc, "junk", [128, D], BF16))
        xring = Ring(P, ph, "xr", 2, [128, D], F32, dma=True)
        ssq = Ring(P, ph, "ssq", 4, [128, 1], F32)
        rstd = Ring(P, ph, "rstd", 4, [128, 1], F32)
        tmp = Ring(P, ph, "tmp", 1, [128, D], F32, dma=True)
        hb = Ring(P, ph, "hb", 2, [128, D], BF16)
        pT = Ring(P, ph, "pT", 1, [128, D], BF16, psum=True)
        hT = [S(SBT(nc, "hT%d" % i, [128, KC, 512], BF16)) for i in range(2)]
        hTb = [[Buf("hT%d_%d" % (i, j)) for j in range(4)] for i in range(2)]
        wsl = Ring(P, ph, "wsl", 2, [128, KC, 512], BF16, dma="pool")
        pmm = Ring(P, ph, "pmm", 4, [128, 512], F32, psum=True)
        pvst = Ring(P, ph, "pvst", 2, [128, 512], BF16, dma=True)
        gt = Ring(P, ph, "gt", 2, [128, 512], F32)
        gst = Ring(P, ph, "gst", 2, [128, 512], BF16, dma=True)
        vst = Ring(P, ph, "vst", 2, [128, 512], BF16, dma=True)
        sqr = Ring(P, ph, "sqr", 2, [128, 512], F32)
        ss8 = Ring(P, ph, "ss8", 2, [128, 8], F32)
        rs8 = Ring(P, ph, "rs8", 2, [128, 8], F32)
        xn = Ring(P, ph, "xn", 2, [128, 512], F32)
        rtmps = [Ring(P, ph, "rtmp%d" % a, 2, [128, 256], F32) for a in range(4)]
        qn = Ring(P, ph, "qn", 4, [128, 512], BF16)
        pq = Ring(P, ph, "pq", 1, [128, 512], BF16, psum=True)
        qst = Ring(P, ph, "qst", 2, [128, 512], BF16, dma=True)

        nwb = tmp.next()
        P.dma("sp", nwb.t[:], norm_w[l].partition_broadcast(128), nwb, [], [nwb])
        for (Gb, Sb_, r) in ((Gx, Sx, 0), (Gc, Sc, 1)):
            P.dma("sp", Sb_.t[:], modd[l, r, 0:D].partition_broadcast(128), Sb_, [], [Sb_])
            P.dma("sp", Gb.t[:], modd[l, r, D:2 * D].partition_broadcast(128), Gb, [], [Gb])
            P.stt(Gb.t[:], Gb.t[:], 1.0, nwb.t[:], ALU.add, ALU.mult, [Gb, nwb], [Gb])

        blks = blocks_of_tiles(True)

        def prep1(T):
            xt = xring.next()
            src = c_src[T * 128:(T + 1) * 128, :] if T < 2 else x_src[(T - 2) * 128:(T - 1) * 128, :]
            P.dma("sp", xt.t[:], src, xt, [], [xt])
            sq_ = ssq.next()
            P.memset(sq_.t[:], 0.0, [], [sq_])
            P.act(junk[:], xt.t[:], AF.Square, [xt, sq_], [sq_], accum=sq_.t[:])
            rs = rstd.next()
            P.act(rs.t[:], sq_.t[:], AF.Ln, [sq_], [rs], scale=1.0 / D, bias=EPS)
            P.act(rs.t[:], rs.t[:], AF.Exp, [rs], [rs], scale=-0.5)
            G_, S_ = (Gc, Sc) if T < 2 else (Gx, Sx)
            tm = tmp.next()
            P.stt(tm.t[:], xt.t[:], rs.t[:, 0:1], G_.t[:], ALU.mult, ALU.mult, [xt, rs, G_], [tm])
            hb_ = hb.next()
            P.tt(hb_.t[:], tm.t[:], S_.t[:], ALU.add, [tm, S_], [hb_])
            return hb_

        def prep2(hb_, slot, pos):
            pt = pT.next()
            P.transposes([(pt.t[:, k * 128:(k + 1) * 128], hb_.t[:, k * 128:(k + 1) * 128]) for k in range(KC)],
                         identb.t[:], [hb_, identb], [pt])
            P.act(hT[slot][:, :, pos * 128:(pos + 1) * 128], pt.t[:].rearrange("p (k t) -> p k t", t=128),
                  AF.Copy, [pt], [hTb[slot][pos]])

        deferred = []
        DEPTH = 2

        def drain(keep):
            while len(deferred) > keep:
                deferred.pop(0)()

        kinds = ["pv", "pv", "pg", "pg", "q", "q", "k", "k", "v", "v", "ag", "ag"]
        for pos, T in enumerate(blks[0]):
            prep2(prep1(T), 0, pos)
        for bi, tiles in enumerate(blks):
            slot = bi % 2
            N = 128 * len(tiles)
            tok0 = tiles[0] * 128
            is_ctx = (bi == 0)
            cbs = list(range(12)) if (with_ctx or not is_ctx) else [6, 7, 8, 9]
            nxt = blks[bi + 1] if bi + 1 < len(blks) else []
            events = {}
            hb_of = {}
            for i, Tn in enumerate(nxt):
                p1 = min(len(cbs) - 1, 2 * i)
                p2 = min(len(cbs) - 1, 2 * i + 2)
                events.setdefault(p1, []).append((float(i), lambda Tn=Tn, i=i: hb_of.__setitem__(i, prep1(Tn))))
                events.setdefault(p2, []).append((i + 1.5, lambda i=i, s_=1 - slot: prep2(hb_of[i], s_, i)))
            hbufs = hTb[slot][:len(tiles)]
            for ci, cb in enumerate(cbs):
                w = wsl.next()
                wsrc = G["w_in_bf"][l][:, cb * 512:(cb + 1) * 512].rearrange("(k p) n -> p k n", p=128)
                P.dma_multi("pool", [(w.t[:, 4 * j:4 * j + 4, :], wsrc[:, 4 * j:4 * j + 4, :]) for j in range(4)],
                            w, [G["wbf_in"][l]], [w])
                kind = kinds[cb]
                if kind in ("pv", "pg", "ag"):
                    for sub in range(4):
                        c = (cb % 2) * 4 + sub
                        drain(DEPTH)
                        ps = pmm.next()
                        P.mm(ps.t[:, 0:N], [(w.t[:, k, sub * 128:(sub + 1) * 128], hT[slot][:, k, 0:N])
                                            for k in range(KC)], [w] + hbufs, [ps])
                        if kind == "pv":
                            st = pvst.next()
                            P.act(st.t[:, 0:N], ps.t[:, 0:N], AF.Copy, [ps], [st])
                            P.dma("sp", pvT[c, :, tok0:tok0 + N], st.t[:, 0:N], st, [st], [])
                        else:
                            g1 = gt.next()
                            P.act(g1.t[:, 0:N], ps.t[:, 0:N], AF.Exp, [ps], [g1], scale=-1.0)
                            P.act(g1.t[:, 0:N], g1.t[:, 0:N], AF.Ln, [g1], [g1], scale=1.0, bias=1.0)
                            P.act(g1.t[:, 0:N], g1.t[:, 0:N], AF.Exp, [g1], [g1], scale=-1.0)
                            st = gst.next()
                            P.tt(st.t[:, 0:N], ps.t[:, 0:N], g1.t[:, 0:N], ALU.mult, [ps, g1], [st])
                            dst = pgT if kind == "pg" else agT
                            P.dma("sp", dst[c, :, tok0:tok0 + N], st.t[:, 0:N], st, [st], [])
                else:
                    h0 = (cb % 2) * 4
                    for pos, T in enumerate(tiles):
                        drain(DEPTH)
                        ps = pmm.next()
                        P.mm(ps.t[:, :], [(hT[slot][:, k, pos * 128:(pos + 1) * 128], w.t[:, k, :])
                                          for k in range(KC)], [w, hTb[slot][pos]], [ps])
                        if kind == "v":
                            st = vst.next()
                            P.act(st.t[:], ps.t[:], AF.Copy, [ps], [st])
                            P.dma("sp", Vs[h0:h0 + 4, :, T, :].rearrange("h p e -> p h e"),
                                  st.t[:].rearrange("p (h e) -> p h e", e=128), st, [st], [])
                            continue
                        wv = (qw if kind == "q" else kw)
                        sq_ = sqr.next()
                        P.act(sq_.t[:], ps.t[:], AF.Square, [ps], [sq_])
                        s8 = ss8.next()
                        P.red(s8.t[:], sq_.t[:].rearrange("p (g d) -> p g d", d=64), [sq_], [s8])
                        r8 = rs8.next()
                        P.act(r8.t[:], s8.t[:], AF.Ln, [s8], [r8], scale=1.0 / 64, bias=EPS)
                        P.act(r8.t[:], r8.t[:], AF.Exp, [r8], [r8], scale=-0.5)
                        xn_ = xn.next()
                        xg = xn_.t[:].rearrange("p (g d) -> p g d", d=64)
                        P.tt(xg, ps.t[:].rearrange("p (g d) -> p g d", d=64),
                             bc(r8.t[:].unsqueeze(2), [128, 8, 64]), ALU.mult, [ps, r8], [xn_])
                        P.tt(xg, xg, bc(wv.t[:, l * 64:(l + 1) * 64].unsqueeze(1), [128, 8, 64]), ALU.mult,
                             [xn_, wv], [xn_])
                        xv = xn_.t[:].rearrange("p (s h t i) -> p s h t i", s=8, h=2, t=2, i=16)
                        x1 = xv[:, :, :, 0, :]
                        x2 = xv[:, :, :, 1, :]
                        cs_ = bc(cosT.t[:, T * 32:(T + 1) * 32].rearrange("p (h i) -> p h i", i=16).unsqueeze(1),
                                 [128, 8, 2, 16])
                        sn_ = bc(sinT.t[:, T * 32:(T + 1) * 32].rearrange("p (h i) -> p h i", i=16).unsqueeze(1),
                                 [128, 8, 2, 16])
                        rts = [r_.next() for r_ in rtmps]
                        rv = [r_.t[:].rearrange("p (s h i) -> p s h i", s=8, h=2, i=16) for r_ in rts]
                        P.tt(rv[0], x1, cs_, ALU.mult, [xn_, cosT], [rts[0]])
                        P.tt(rv[1], x2, cs_, ALU.mult, [xn_, cosT], [rts[1]])
                        P.tt(rv[2], x2, sn_, ALU.mult, [xn_, sinT], [rts[2]])
                        P.tt(rv[3], x1, sn_, ALU.mult, [xn_, sinT], [rts[3]])
                        qn_ = qn.next()
                        qv = qn_.t[:].rearrange("p (s h t i) -> p s h t i", s=8, h=2, t=2, i=16)
                        P.tt(qv[:, :, :, 0, :], rv[0], rv[2], ALU.subtract, [rts[0], rts[2]], [qn_])
                        P.tt(qv[:, :, :, 1, :], rv[1], rv[3], ALU.add, [rts[1], rts[3]], [qn_])
                        if kind == "q":
                            if T < 2:
                                dst = qcT[h0:h0 + 4, :, T * 128:(T + 1) * 128]
                            else:
                                dst = qT[h0:h0 + 4, :, (T - 2) * 128:(T - 1) * 128]
                        else:
                            dst = kT[h0:h0 + 4, :, T * 128:(T + 1) * 128]

                        def stage(qn_=qn_, dst=dst):
                            pq_ = pq.next()
                            P.transposes([(pq_.t[:, hh * 128:(hh + 1) * 128], qn_.t[:, hh * 128:(hh + 1) * 128])
                                          for hh in range(4)], identb.t[:], [qn_, identb], [pq_])
                            st = qst.next()
                            P.act(st.t[:], pq_.t[:], AF.Copy, [pq_], [st])
                            P.dma("sp", dst.rearrange("h p t -> p h t"),
                                  st.t[:].rearrange("p (h t) -> p h t", t=128), st, [st], [])

                        deferred.append(stage)
                for _key, fn_ in sorted(events.get(ci, []), key=lambda e_: e_[0]):
                    fn_()
        drain(0)


def phase_pool(nc, P, l, with_ctx, G):
    pool_w, pvT, pgT, mT, edge, psc = G["pool_w"], G["pvT"], G["pgT"], G["mT"], G["edge"], G["psc"]
    with ExitStack() as ph:
        S = ph.enter_context
        WP = L + 16
        pwb = Buf("pwb", S(SBT(nc, "pwb", [128, 4, 2, 256], BF16)), P.new_dsem())
        vb = Ring(P, ph, "vb", 2, [128, L], BF16, dma=True)
        vp = Ring(P, ph, "vp", 2, [128, WP], F32)
        sr = Ring(P, ph, "sr", 3, [128, WP], F32)
        plr = Ring(P, ph, "plr", 4, [128, L], BF16)
        pgr = Ring(P, ph, "pgr", 2, [128, L], BF16, dma=True)
        pst = Ring(P, ph, "pst", 2, [128, 512], BF16, dma=True)
        etr = Ring(P, ph, "etr", 2, [128, 16], F32)
        pp = Ring(P, ph, "pp", 4, [128, 512], F32, psum=True)
        pwsrc = pool_w[l].rearrange("g (c p) d -> p g c d", p=128)
        P.dma_multi("pool", [(pwb.t[:, g_], pwsrc[:, g_]) for g_ in range(4)], pwb, [], [pwb])
        segs = ([(0, C)] if with_ctx else []) + [(C, L)]
        for g in range(4):
            w = WINS[g]
            for (t0, Ls) in segs:
                Wd = Ls + 16
                pls = []
                for cc in range(2):
                    c = g * 2 + cc
                    v = vb.next()
                    P.dma("sp", v.t[:, 0:Ls], pvT[c, :, t0:t0 + Ls], v, [], [v])
                    vp_ = vp.next()
                    P.memset(vp_.t[:, 0:8], 0.0, [], [vp_])
                    P.memset(vp_.t[:, 8 + Ls:16 + Ls], 0.0, [], [vp_])
                    P.copy(vp_.t[:, 8:8 + Ls], v.t[:, 0:Ls], [v, vp_], [vp_])
                    cur = vp_
                    ln = Wd
                    for s in range(g + 1):
                        sh = 1 << s
                        nx = sr.next()
                        ln = ln - sh
                        P.tt(nx.t[:, 0:ln], cur.t[:, 0:ln], cur.t[:, sh:sh + ln], ALU.add, [cur], [nx])
                        cur = nx
                    off = 8 - w // 2
                    pl_ = plr.next()
                    P.stt(pl_.t[:, 0:Ls], cur.t[:, off:off + Ls], 1.0 / w, vp_.t[:, 8:8 + Ls], ALU.mult, ALU.subtract,
                          [cur, vp_], [pl_])
                    et = etr.next()
                    P.tt(et.t[:, 0:8], cur.t[:, off:off + 8], edge.t[:, g * 16:g * 16 + 8], ALU.mult,
                         [cur, edge], [et])
                    P.tt(et.t[:, 8:16], cur.t[:, off + Ls - 8:off + Ls], edge.t[:, g * 16 + 8:g * 16 + 16], ALU.mult,
                         [cur, edge, et], [et])
                    P.tt(pl_.t[:, 0:8], et.t[:, 0:8], vp_.t[:, 8:16], ALU.subtract, [et, vp_, pl_], [pl_])
                    P.tt(pl_.t[:, Ls - 8:Ls], et.t[:, 8:16], vp_.t[:, Ls:Ls + 8], ALU.subtract, [et, vp_, pl_], [pl_])
                    pls.append(pl_)
                for dt in range(2):
                    c2 = g * 2 + dt
                    pg_ = pgr.next()
                    P.dma("sp", pg_.t[:, 0:Ls], pgT[c2, :, t0:t0 + Ls], pg_, [], [pg_])
                    for blk in range(0, Ls, 512):
                        n = min(512, Ls - blk)
                        ps = pp.next()
                        P.mm(ps.t[:, 0:n], [(pwb.t[:, g, cc, dt * 128:(dt + 1) * 128], pls[cc].t[:, blk:blk + n])
                                            for cc in range(2)], [pwb] + pls, [ps])
                        st = pst.next()
                        P.stt(st.t[:, 0:n], ps.t[:, 0:n], psc.t[:, l * 8 + c2:l * 8 + c2 + 1], pg_.t[:, blk:blk + n],
                              ALU.mult, ALU.mult, [ps, psc, pg_], [st])
                        P.dma("sp", mT[c2, :, t0 + blk:t0 + blk + n], st.t[:, 0:n], st, [st], [])


def make_pool_tasks(nc, P, l, with_ctx, G, ph, pB):
    pool_w, pvT, pgT, mT, edge, psc = G["pool_w"], G["pvT"], G["pgT"], G["mT"], G["edge"], G["psc"]
    SEG = 1024
    WP = SEG + 16
    pwb = Buf("pwb", ph.enter_context(SBT(nc, "pwb", [128, 4, 2, 256], BF16)), P.new_dsem("pool"))
    vb = Ring(P, ph, "vb", 2, [128, WP], BF16, dma=True)
    vp = Ring(P, ph, "vp", 2, [128, WP], F32)
    sr = Ring(P, ph, "sr", 3, [128, WP], F32)
    plr = Ring(P, ph, "plr", 4, [128, SEG], BF16)
    pgr = Ring(P, ph, "pgr", 2, [128, SEG], BF16, dma=True)
    pst = Ring(P, ph, "pst", 2, [128, 512], BF16, dma=True)
    etr = Ring(P, ph, "etr", 2, [128, 16], F32)
    pwsrc = pool_w[l].rearrange("g (c p) d -> p g c d", p=128)
    P.dma_multi("pool", [(pwb.t[:, g_], pwsrc[:, g_]) for g_ in range(4)], pwb, [], [pwb])
    seqs = ([(0, C)] if with_ctx else []) + [(C, L)]
    units = []
    for g in range(4):
        for (q0, Lq) in seqs:
            for s0 in range(q0, q0 + Lq, SEG):
                units.append((g, q0, Lq, s0, min(SEG, q0 + Lq - s0)))

    class _Rec:
        def __init__(self):
            self.ops = []

        def __getattr__(self, name):
            def f(*a_, **k_):
                self.ops.append((name, a_, k_))
            return f

    R_ = _Rec()
    results = {}

    def dve_part(g, q0, Lq, s0, Ls, out):
        w = WINS[g]
        Wd = Ls + 16
        base = s0 - 8
        lo = max(q0, base)
        hi = min(q0 + Lq, s0 + Ls + 8)
        for cc in range(2):
            c = g * 2 + cc
            v = vb.next()
            R_.dma("sp", v.t[:, lo - base:hi - base], pvT[c, :, lo:hi], v, [], [v])
            vp_ = vp.next()
            if lo > base:
                R_.memset(vp_.t[:, 0:lo - base], 0.0, [], [vp_])
            if hi < base + Wd:
                R_.memset(vp_.t[:, hi - base:Wd], 0.0, [], [vp_])
            R_.copy(vp_.t[:, lo - base:hi - base], v.t[:, lo - base:hi - base], [v, vp_], [vp_])
            cur = vp_
            ln = Wd
            for s_ in range(g + 1):
                sh = 1 << s_
                nx = sr.next()
                ln = ln - sh
                R_.tt(nx.t[:, 0:ln], cur.t[:, 0:ln], cur.t[:, sh:sh + ln], ALU.add, [cur], [nx])
                cur = nx
            off = 8 - w // 2
            pl_ = plr.next()
            R_.stt(pl_.t[:, 0:Ls], cur.t[:, off:off + Ls], 1.0 / w, vp_.t[:, 8:8 + Ls], ALU.mult, ALU.subtract,
                  [cur, vp_], [pl_])
            if s0 == q0 or s0 + Ls == q0 + Lq:
                et = etr.next()
                if s0 == q0:
                    R_.tt(et.t[:, 0:8], cur.t[:, off:off + 8], edge.t[:, g * 16:g * 16 + 8], ALU.mult,
                         [cur, edge, et], [et])
                    R_.tt(pl_.t[:, 0:8], et.t[:, 0:8], vp_.t[:, 8:16], ALU.subtract, [et, vp_, pl_], [pl_])
                if s0 + Ls == q0 + Lq:
                    R_.tt(et.t[:, 8:16], cur.t[:, off + Ls - 8:off + Ls], edge.t[:, g * 16 + 8:g * 16 + 16],
                         ALU.mult, [cur, edge, et], [et])
                    R_.tt(pl_.t[:, Ls - 8:Ls], et.t[:, 8:16], vp_.t[:, Ls:Ls + 8], ALU.subtract,
                         [et, vp_, pl_], [pl_])
            out.append(pl_)

    def pe_part(g, q0, Lq, s0, Ls, pls):
        for dt in range(2):
            c2 = g * 2 + dt
            pg_ = pgr.next()
            R_.dma("sp", pg_.t[:, 0:Ls], pgT[c2, :, s0:s0 + Ls], pg_, [], [pg_])
            for blk in range(0, Ls, 512):
                n = min(512, Ls - blk)
                R_.mm(pB.t[:, 0:n], [(pwb.t[:, g, cc, dt * 128:(dt + 1) * 128], pls[cc].t[:, blk:blk + n])
                                    for cc in range(2)], [pwb] + pls, [pB])
                st = pst.next()
                R_.stt(st.t[:, 0:n], pB.t[:, 0:n], psc.t[:, l * 8 + c2:l * 8 + c2 + 1], pg_.t[:, blk:blk + n],
                      ALU.mult, ALU.mult, [pB, psc, pg_], [st])
                R_.dma("sp", mT[c2, :, s0 + blk:s0 + blk + n], st.t[:, 0:n], st, [st], [])

    for i, u in enumerate(units):
        if i > 0:
            pe_part(*units[i - 1], results[i - 1])
        outl = []
        dve_part(*u, outl)
        results[i] = outl
    pe_part(*units[-1], results[len(units) - 1])
    micro = []
    ops = R_.ops
    j = 0
    while j < len(ops):
        grp = [ops[j]]
        if ops[j][0] == "mm" and j + 1 < len(ops):
            grp.append(ops[j + 1])
        j += len(grp)

        def run(grp=grp):
            for (name, a_, k_) in grp:
                getattr(P, name)(*a_, **k_)
        micro.append(run)
    return micro


def phase_attn(nc, P, l, with_ctx, G):
    qT, qcT, kT, Vs, agT, mT = G["qT"], G["qcT"], G["kT"], G["Vs"], G["agT"], G["mT"]
    onesb, neglam, sws, selb = G["onesb"], G["neglam"], G["sws"], G["selb"]
    with ExitStack() as ph:
        kTh = Ring(P, ph, "kTh", 2, [128, LT], BF16, dma=True)
        Vh = Ring(P, ph, "Vh", 2, [128, NT * 128], BF16, dma=True)
        qTh = Ring(P, ph, "qTh", 2, [128, L], BF16, dma=True)
        agh = Ring(P, ph, "agh", 2, [128, LT], BF16, dma=True)
        qch = Ring(P, ph, "qch", 2, [128, C], BF16, dma=True)
        Eb = Ring(P, ph, "Eb", 3, [128, 1024], BF16)
        orA = Ring(P, ph, "orA", 2, [128, 512], F32)
        orB = Ring(P, ph, "orB", 2, [128, 512], F32)
        zraw = Ring(P, ph, "zraw", 2, [64, 512], F32)
        rzf = Ring(P, ph, "rzf", 2, [64, 512], F32)
        rzh = Ring(P, ph, "rzh", 2, [64, 512], BF16)
        rzd = Ring(P, ph, "rzd", 2, [64, 512], F32)
        rzl = Ring(P, ph, "rzl", 2, [64, 512], BF16)
        t0r = Ring(P, ph, "t0r", 2, [128, 512], F32)
        t1r = Ring(P, ph, "t1r", 2, [128, 512], F32)
        ob = Ring(P, ph, "ob", 2, [128, 512], F32)
        sqb = Ring(P, ph, "sqb", 2, [128, 512], BF16)
        rsb = Ring(P, ph, "rsb", 2, [128, 512], F32)
        ub = Ring(P, ph, "ub", 2, [128, 512], F32)
        ost = Ring(P, ph, "ost", 2, [128, 512], BF16, dma=True)
        pS = Ring(P, ph, "pS", 2, [128, 1024], F32, psum=True)
        pO = Ring(P, ph, "pO", 1, [128, 1024], F32, psum=True).next()
        pZ = Ring(P, ph, "pZ", 1, [128, 512], F32, psum=True).next()
        pB = Ring(P, ph, "pB", 1, [128, 512], F32, psum=True).next()

        def v2(ap, N):
            return ap.rearrange("p (m n) -> p m n", m=2)[:, :, 0:N]

        pool_tasks = make_pool_tasks(nc, P, l, with_ctx, G, ph, pB)
        blk_count = [0]

        stages = []

        def run_stages(upto):
            while stages and stages[0][0] <= upto:
                stages.pop(0)[1]()

        for h in range(H):
            k_ = kTh.next()
            P.dma("sp", k_.t[:], kT[h], k_, [], [k_])
            q_ = qTh.next()
            P.dma("sp", q_.t[:], qT[h], q_, [], [q_])
            v_ = Vh.next()
            P.dma("sp", v_.t[:], Vs[h].rearrange("p t e -> p (t e)"), v_, [], [v_])
            a_ = agh.next()
            P.dma("sp", a_.t[:], agT[h], a_, [], [a_])
            if with_ctx:
                qc_ = qch.next()
                P.dma("sp", qc_.t[:], qcT[h], qc_, [], [qc_])
            blocks = []
            if with_ctx:
                blocks.append((qc_, qc_.t[:, 0:C], 0, C, [0, 1]))
            for qb in range(8):
                blocks.append((q_, q_.t[:, qb * 512:(qb + 1) * 512], C + qb * 512, 512, list(range(NT))))
            for (qbuf, qsrc, tok0, N, kts) in blocks:
                nk = len(kts)
                if nk < 26:
                    run_stages(10 ** 9)

                def S_op(kt):
                    ps = pS.next()
                    P.mm_multi([(ps.t[:, 0:N], k_.t[0:64, kt * 128:(kt + 1) * 128], qsrc[0:64, :], True, True, None),
                                (ps.t[:, 512:512 + N], k_.t[64:128, kt * 128:(kt + 1) * 128], qsrc[64:128, :],
                                 True, True, None)], [k_, qbuf], [ps])
                    return ps

                ps_list = {0: S_op(kts[0])}
                if nk > 1:
                    ps_list[1] = S_op(kts[1])
                for i, kt in enumerate(kts):
                    ps = ps_list.pop(i)
                    e_ = Eb.next()
                    P.act(v2(e_.t[:], N), v2(ps.t[:], N), AF.Exp, [ps], [e_], scale=0.125)
                    run_stages(i)
                    if i + 2 < nk:
                        ps_list[i + 2] = S_op(kts[i + 2])
                    first = (i == 0)
                    last = (i == nk - 1)
                    vt = v_.t[:, kt * 128:(kt + 1) * 128]
                    items = [(pO.t[:, 0:N], vt, e_.t[:, 0:N], first, last, None),
                             (pO.t[:, 512:512 + N], vt, e_.t[:, 512:512 + N], first, last, None),
                             (pZ.t[0:32, 0:N], onesb.t[:, 0:32], e_.t[:, 0:N], first, last, (0, 0)),
                             (pZ.t[32:64, 0:N], onesb.t[:, 0:32], e_.t[:, 512:512 + N], first, last, (0, 32))]
                    P.mm_multi(items, [e_, v_, onesb], [pO, pZ] if (first or last) else [])
                    if i % 3 == 2 and pool_tasks:
                        pool_tasks.pop(0)()
                run_stages(10 ** 9)
                oa = orA.next()
                P.act(oa.t[:, 0:N], pO.t[:, 0:N], AF.Copy, [pO], [oa])
                obb = orB.next()
                P.copy(obb.t[:, 0:N], pO.t[:, 512:512 + N], [pO], [obb])
                zr = zraw.next()
                P.act(zr.t[0:64, 0:N], pZ.t[0:64, 0:N], AF.Copy, [pZ], [zr])

                def stageA1(zr=zr, N=N):
                    rf = rzf.next()
                    P.recip(rf.t[0:64, 0:N], zr.t[0:64, 0:N], [zr], [rf])
                    rh = rzh.next()
                    P.copy(rh.t[0:64, 0:N], rf.t[0:64, 0:N], [rf], [rh])
                    rd = rzd.next()
                    P.tt(rd.t[0:64, 0:N], rf.t[0:64, 0:N], rh.t[0:64, 0:N], ALU.subtract, [rf, rh], [rd])
                    rl = rzl.next()
                    P.copy(rl.t[0:64, 0:N], rd.t[0:64, 0:N], [rd], [rl])
                    return rh, rl

                def stageA2(carry, oa=oa, N=N):
                    rh, rl = carry["a1"]
                    P.mm(pB.t[:, 0:N], [(selb.t[0:64, 0:128], rh.t[0:64, 0:N]), (selb.t[0:64, 0:128], rl.t[0:64, 0:N])],
                         [selb, rh, rl], [pB])
                    t0 = t0r.next()
                    P.tt(t0.t[:, 0:N], oa.t[:, 0:N], pB.t[:, 0:N], ALU.mult, [oa, pB], [t0])
                    return t0

                def stageB(carry, obb=obb, N=N):
                    rh, rl = carry["a1"]
                    t0 = carry["a2"]
                    P.mm(pB.t[:, 0:N], [(selb.t[0:64, 128:256], rh.t[0:64, 0:N]),
                                        (selb.t[0:64, 128:256], rl.t[0:64, 0:N])], [selb, rh, rl], [pB])
                    t1 = t1r.next()
                    P.tt(t1.t[:, 0:N], obb.t[:, 0:N], pB.t[:, 0:N], ALU.mult, [obb, pB], [t1])
                    o_ = ob.next()
                    P.stt(o_.t[:, 0:N], t1.t[:, 0:N], neglam.t[:, l:l + 1], t0.t[:, 0:N], ALU.mult, ALU.add,
                          [t1, t0, neglam], [o_])
                    sq_ = sqb.next()
                    P.tt(sq_.t[:, 0:N], o_.t[:, 0:N], o_.t[:, 0:N], ALU.mult, [o_], [sq_])
                    return o_, sq_

                def stageC1(carry, N=N):
                    o_, sq_ = carry["b"]
                    P.mm(pB.t[:, 0:N], [(onesb.t[:], sq_.t[:, 0:N])], [sq_, onesb], [pB])
                    rs_ = rsb.next()
                    P.act(rs_.t[:, 0:N], pB.t[:, 0:N], AF.Ln, [pB], [rs_], scale=1.0 / 128, bias=EPS)
                    P.act(rs_.t[:, 0:N], rs_.t[:, 0:N], AF.Exp, [rs_], [rs_], scale=-0.5)
                    return rs_

                def stageC2(carry, N=N, tok0=tok0, a_=a_, h=h):
                    o_, sq_ = carry["b"]
                    rs_ = carry["c1"]
                    u_ = ub.next()
                    P.stt(u_.t[:, 0:N], o_.t[:, 0:N], sws.t[:, l:l + 1], rs_.t[:, 0:N], ALU.mult, ALU.mult,
                          [o_, sws, rs_], [u_])
                    st = ost.next()
                    P.tt(st.t[:, 0:N], u_.t[:, 0:N], a_.t[:, tok0:tok0 + N], ALU.mult, [u_, a_], [st])
                    P.dma("sp", mT[8 + h, :, tok0:tok0 + N], st.t[:, 0:N], st, [st], [])

                carry = {}

                def mk(key, f, carry=carry):
                    def g():
                        carry[key] = f() if key == "a1" else f(carry)
                    return g

                stages.extend([(2, mk("a1", stageA1)), (8, mk("a2", stageA2)), (13, mk("b", stageB)),
                               (18, mk("c1", stageC1)), (23, mk("c2", stageC2))])
        run_stages(10 ** 9)
        while pool_tasks:
            pool_tasks.pop(0)()


def phase_out(nc, P, l, x_src, c_src, with_ctx, G):
    modd, w_out, mT, y, ctx1 = G["modd"], G["w_out"], G["mT"], G["y"], G["ctx1"]
    with ExitStack() as ph:
        S = ph.enter_context
        wo = Buf("wo", S(SBT(nc, "wo", [128, KC, D], BF16)), P.new_dsem("pool"))
        gx = Buf("gx", S(SBT(nc, "gx", [128, D], F32)), P.new_dsem())
        gc = Buf("gc", S(SBT(nc, "gc", [128, D], F32)), P.new_dsem())
        mTb = Ring(P, ph, "mTb", 2, [128, KC, 512], BF16, dma=True)
        xring = Ring(P, ph, "xo", 2, [128, D], F32, dma=True)
        tr = Ring(P, ph, "tr", 2, [128, 512], F32)
        ot = Ring(P, ph, "ot", 2, [128, D], F32, dma=True)
        po = Ring(P, ph, "po", 4, [128, 512], F32, psum=True)
        wosrc = G["w_out_bf"][l].rearrange("(k p) n -> p k n", p=128)
        P.dma_multi("pool", [(wo.t[:, 2 * j:2 * j + 2, :], wosrc[:, 2 * j:2 * j + 2, :]) for j in range(8)],
                    wo, [G["wbf_out"][l]], [wo])
        P.dma("sp", gx.t[:], modd[l, 0, 2 * D:3 * D].partition_broadcast(128), gx, [], [gx])
        P.dma("sp", gc.t[:], modd[l, 1, 2 * D:3 * D].partition_broadcast(128), gc, [], [gc])
        blks = blocks_of_tiles(True)
        if not with_ctx:
            blks = blks[1:]
        for tiles in blks:
            N = 128 * len(tiles)
            tok0 = tiles[0] * 128
            mb = mTb.next()
            msrc = mT[:, :, tok0:tok0 + N].rearrange("c p t -> p c t")
            P.dma_multi("sp", [(mb.t[:, 4 * j:4 * j + 4, 0:N], msrc[:, 4 * j:4 * j + 4, :]) for j in range(4)],
                        mb, [], [mb])
            for pos, T in enumerate(tiles):
                xt = xring.next()
                if T < 2:
                    src = c_src[T * 128:(T + 1) * 128, :]
                    dst = ctx1[T * 128:(T + 1) * 128, :]
                    g_ = gc
                else:
                    src = x_src[(T - 2) * 128:(T - 1) * 128, :]
                    dst = y[(T - 2) * 128:(T - 1) * 128, :]
                    g_ = gx
                P.dma("sp", xt.t[:], src, xt, [], [xt])
                ot_ = ot.next()
                for n in range(4):
                    ps = po.next()
                    P.mm(ps.t[:, :], [(mb.t[:, k, pos * 128:(pos + 1) * 128], wo.t[:, k, n * 512:(n + 1) * 512])
                                      for k in range(KC)], [mb, wo], [ps])
                    t_ = tr.next()
                    P.tt(t_.t[:], ps.t[:], g_.t[:, n * 512:(n + 1) * 512], ALU.mult, [ps, g_], [t_])
                    P.tt(ot_.t[:, n * 512:(n + 1) * 512], t_.t[:], xt.t[:, n * 512:(n + 1) * 512], ALU.add,
                         [t_, xt, ot_], [ot_])
                P.dma("sp", dst, ot_.t[:], ot_, [ot_], [])


def _consts():
    ident = np.eye(128, dtype=np.float32)
    inv = (np.float32(10000.0) ** (-np.arange(16, dtype=np.float32) / np.float32(16))).astype(np.float32)
    cos = np.ones((128, NT, 32), np.float32)
    sin = np.zeros((128, NT, 32), np.float32)
    t = (np.arange(NT - 2)[None, :] * 128 + np.arange(128)[:, None])
    row = (t // 64).astype(np.float32)
    col = (t % 64).astype(np.float32)
    ar = row[:, :, None] * inv
    ac = col[:, :, None] * inv
    cos[:, 2:, 0:16] = np.cos(ar)
    cos[:, 2:, 16:32] = np.cos(ac)
    sin[:, 2:, 0:16] = np.sin(ar)
    sin[:, 2:, 16:32] = np.sin(ac)
    edge = np.zeros((4, 16), np.float32)
    for g, w in enumerate(WINS):
        for i in range(8):
            edge[g, i] = 1.0 / min(w, i + w // 2)
            edge[g, 8 + i] = 1.0 / min(w, (8 - i) + w // 2)
    return ident, cos.reshape(128, NT * 32), sin.reshape(128, NT * 32), edge.reshape(64)


def make_in_maps(x, c, ctx, c_ctx, norm_w, w_ada, b_ada, w_in, pool_w, pool_scale, q_norm_w, k_norm_w,
                 lambda_q1, lambda_k1, lambda_q2, lambda_k2, subln_w, w_out):
    f = lambda a: np.ascontiguousarray(np.asarray(a, dtype=np.float32))
    ident, cos, sin, edge = _consts()
    pscT = f(np.asarray(pool_scale).reshape(NLAYERS, 8, 128).transpose(2, 0, 1).reshape(128, 16))
    swT = f(np.asarray(subln_w).T)
    lams = f(np.stack([np.asarray(lambda_q1), np.asarray(lambda_k1), np.asarray(lambda_q2),
                       np.asarray(lambda_k2)], axis=1).reshape(-1))
    shared = {
        "norm_w": f(norm_w), "w_ada": f(w_ada), "b_ada": f(b_ada), "w_in": f(w_in), "pool_w": f(pool_w),
        "pscT": pscT, "qw": f(np.asarray(q_norm_w).reshape(-1)), "kw": f(np.asarray(k_norm_w).reshape(-1)),
        "lams": lams, "swT": swT, "w_out": f(w_out), "ident": ident, "rope_cos": cos, "rope_sin": sin,
        "edge_rc": edge,
    }
    x = np.asarray(x)
    c = np.asarray(c)
    ctx = np.asarray(ctx)
    c_ctx = np.asarray(c_ctx)
    maps = []
    for b in range(x.shape[0]):
        cond = np.stack([c[b], c_ctx], axis=-1).reshape(KC, 128, 2).transpose(1, 0, 2).reshape(128, 32)
        m = dict(shared)
        m["x"] = f(x[b])
        m["ctx"] = f(ctx[b])
        m["cond"] = f(cond)
        maps.append(m)
    return maps


def kernel(**inputs):
    maps = make_in_maps(**inputs)
    nc = build()
    res = run_bass_kernel_spmd(nc, maps, core_ids=list(range(len(maps))))
    return np.stack([np.asarray(r["y"], dtype=np.float32) for r in res.results], axis=0)
```
